# Optimizing a Trainium2 kernel written in Bass

```python
import jax, jax.numpy as jnp
from jax import lax
import numpy as np

D_MODEL = 1024
BATCH = 32
SEQ = 2048
DEPTH = 1

PLE_DIM = 256
MIX_WIDTH = D_MODEL
ATTN_WIDTH = MIX_WIDTH // 2
CONV_WIDTH = MIX_WIDTH - ATTN_WIDTH
V_DIM = 64
NOPE_DIM = 64
ROPE_DIM = 32
N_ATTN_HEADS = ATTN_WIDTH // V_DIM
Q_RANK = D_MODEL // 4
KV_RANK = D_MODEL // 4
N_CONV_GROUPS = 8
CONV_K = 3
IN_COLS = Q_RANK + KV_RANK + ROPE_DIM + 3 * CONV_WIDTH
D_FF = ((8 * D_MODEL // 3 + 255) // 256) * 256
ROPE_THETA = 10000.0
Q_BLOCK = 128
EPS = 1e-6

kernel_name = "hybrid_mla_shortconv_parallel_layer"


def rmsnorm(x, g):
    xf = x.astype(jnp.float32)
    y = xf * lax.rsqrt(jnp.mean(xf * xf, axis=-1, keepdims=True) + EPS)
    return (y * g.astype(jnp.float32)).astype(x.dtype)


def rope_tables(positions, dtype):
    half = ROPE_DIM // 2
    inv_freq = 1.0 / (ROPE_THETA ** (jnp.arange(half, dtype=jnp.float32) / half))
    ang = positions.astype(jnp.float32)[..., None] * inv_freq
    return jnp.cos(ang).astype(dtype), jnp.sin(ang).astype(dtype)


def apply_rope(t, cos, sin):
    t1, t2 = jnp.split(t, 2, axis=-1)
    return jnp.concatenate([t1 * cos - t2 * sin, t2 * cos + t1 * sin], axis=-1)


def causal_block_attention(q, k, v):
    B, S, H, Dq = q.shape
    Dv = v.shape[-1]
    nb = S // Q_BLOCK
    scale = Dq ** -0.5
    qb = q.reshape(B, nb, Q_BLOCK, H, Dq).transpose(1, 0, 2, 3, 4)
    k_idx = jnp.arange(S)

    def one_block(args):
        q_blk, blk = args
        q_idx = blk * Q_BLOCK + jnp.arange(Q_BLOCK)
        s = jnp.einsum('bqhd,bkhd->bhqk', q_blk, k).astype(jnp.float32) * scale
        s = jnp.where(k_idx[None, :] <= q_idx[:, None], s, jnp.finfo(jnp.float32).min)
        pr = jax.nn.softmax(s, axis=-1)
        return jnp.einsum('bhqk,bkhd->bqhd', pr.astype(v.dtype), v)

    out = lax.map(one_block, (qb, jnp.arange(nb)))
    return out.transpose(1, 0, 2, 3, 4).reshape(B, S, H * Dv)


def mla(c_q_raw, c_kv_raw, k_rope_raw, positions, q_norm, kv_norm, w_uq, w_ukv):
    B, S, _ = c_q_raw.shape
    H = N_ATTN_HEADS
    q = (rmsnorm(c_q_raw, q_norm) @ w_uq).reshape(B, S, H, NOPE_DIM + ROPE_DIM)
    kv = (rmsnorm(c_kv_raw, kv_norm) @ w_ukv).reshape(B, S, H, NOPE_DIM + V_DIM)
    k_nope, v = kv[..., :NOPE_DIM], kv[..., NOPE_DIM:]
    cos, sin = rope_tables(positions, q.dtype)
    q_rope = apply_rope(q[..., NOPE_DIM:], cos[:, :, None, :], sin[:, :, None, :])
    k_rope = apply_rope(k_rope_raw, cos, sin)
    q_full = jnp.concatenate([q[..., :NOPE_DIM], q_rope], axis=-1)
    k_full = jnp.concatenate(
        [k_nope, jnp.broadcast_to(k_rope[:, :, None, :], (B, S, H, ROPE_DIM))], axis=-1)
    return causal_block_attention(q_full, k_full, v)


def short_gated_conv(b_gate, c_gate, xv, conv_w):
    S = xv.shape[1]
    u = c_gate * xv
    u_pad = jnp.pad(u, ((0, 0), (CONV_K - 1, 0), (0, 0)))
    y = conv_w[0] * u_pad[:, CONV_K - 1:CONV_K - 1 + S]
    for k in range(1, CONV_K):
        y = y + conv_w[k] * u_pad[:, CONV_K - 1 - k:CONV_K - 1 - k + S]
    return b_gate * y


def setup_inputs(seed: int = 0) -> dict:
    key = jax.random.key(seed)
    ks = jax.random.split(key, 24)
    f32 = jnp.float32

    def w(k, shape, fan_in):
        return jax.random.normal(k, shape, f32) * (fan_in ** -0.5)

    def gain(k, shape):
        return 1.0 + 0.05 * jax.random.normal(k, shape, f32)

    L = DEPTH
    x = jax.random.normal(ks[0], (BATCH, SEQ, D_MODEL), f32)
    p = jax.random.normal(ks[1], (DEPTH, BATCH, SEQ, PLE_DIM), f32)
    offset = jax.random.randint(ks[2], (BATCH, 1), 0, 4096)
    positions = (offset + jnp.arange(SEQ, dtype=jnp.int32)[None, :]).astype(jnp.int32)
    return {
        "x": x,
        "p": p,
        "positions": positions,
        "mix_pre_norm": gain(ks[3], (L, D_MODEL)),
        "w_in": w(ks[4], (L, D_MODEL, IN_COLS), D_MODEL),
        "q_norm": gain(ks[5], (L, Q_RANK)),
        "kv_norm": gain(ks[6], (L, KV_RANK)),
        "w_uq": w(ks[7], (L, Q_RANK, N_ATTN_HEADS * (NOPE_DIM + ROPE_DIM)), Q_RANK),
        "w_ukv": w(ks[8], (L, KV_RANK, N_ATTN_HEADS * (NOPE_DIM + V_DIM)), KV_RANK),
        "conv_w": w(ks[9], (L, CONV_K, CONV_WIDTH), CONV_K),
        "attn_group_norm": gain(ks[10], (L, ATTN_WIDTH)),
        "conv_group_norm": gain(ks[11], (L, CONV_WIDTH)),
        "w_out": w(ks[12], (L, MIX_WIDTH, D_MODEL), MIX_WIDTH),
        "mix_post_norm": gain(ks[13], (L, D_MODEL)),
        "ffn_pre_norm": gain(ks[14], (L, D_MODEL)),
        "w_gate": w(ks[15], (L, D_MODEL, D_FF), D_MODEL),
        "w_up": w(ks[16], (L, D_MODEL, D_FF), D_MODEL),
        "w_down": w(ks[17], (L, D_FF, D_MODEL), D_FF),
        "ffn_post_norm": gain(ks[18], (L, D_MODEL)),
        "w_ple_proj": w(ks[19], (L, PLE_DIM, D_MODEL), PLE_DIM),
        "ple_norm": gain(ks[20], (L, D_MODEL)),
        "w_ple_gate": w(ks[21], (L, D_MODEL, D_MODEL), D_MODEL),
    }


def reference(x, p, positions, mix_pre_norm, w_in, q_norm, kv_norm, w_uq, w_ukv,
              conv_w, attn_group_norm, conv_group_norm, w_out, mix_post_norm,
              ffn_pre_norm, w_gate, w_up, w_down, ffn_post_norm,
              w_ple_proj, ple_norm, w_ple_gate):
    o1 = Q_RANK
    o2 = o1 + KV_RANK
    o3 = o2 + ROPE_DIM
    o4 = o3 + CONV_WIDTH
    o5 = o4 + CONV_WIDTH
    for i in range(DEPTH):
        h = rmsnorm(x, mix_pre_norm[i])
        z = h @ w_in[i]
        a = mla(z[..., :o1], z[..., o1:o2], z[..., o2:o3], positions,
                q_norm[i], kv_norm[i], w_uq[i], w_ukv[i])
        c = short_gated_conv(z[..., o3:o4], z[..., o4:o5], z[..., o5:], conv_w[i])
        m = jnp.concatenate([rmsnorm(a, attn_group_norm[i]),
                             rmsnorm(c, conv_group_norm[i])], axis=-1)
        x = x + rmsnorm(m @ w_out[i], mix_post_norm[i])
        h = rmsnorm(x, ffn_pre_norm[i])
        f = (jax.nn.silu(h @ w_gate[i]) * (h @ w_up[i])) @ w_down[i]
        x = x + rmsnorm(f, ffn_post_norm[i])
        e = rmsnorm(p[i] @ w_ple_proj[i], ple_norm[i])
        x = x + jax.nn.sigmoid(x @ w_ple_gate[i]) * e
    return x
```

```python
import math
from contextlib import ExitStack

import numpy as np

import concourse.bass as bass
import concourse.mybir as mybir
from concourse.bass_utils import run_bass_kernel_spmd

F32 = mybir.dt.float32
BF16 = mybir.dt.bfloat16
I32 = mybir.dt.int32
AF = mybir.ActivationFunctionType
ALU = mybir.AluOpType

NCORES = 8
NSEQ = 4
S = 2048
D = 1024
PLE = 256
CH = 512
NCH = S // CH
DFF = 2816
NH = 8
EPS = 1e-6
QSCALE = 96.0 ** -0.5
O1, O2, O3, O4, O5 = 256, 512, 544, 1056, 1568

ENGS = ("pe", "act", "dve", "pool", "sp")
import os as _os
KSKIP = _os.environ.get("KSKIP", "").split(",")


class T:
    __slots__ = ("name", "last_w", "readers")

    def __init__(self, name=""):
        self.name = name
        self.last_w = None
        self.readers = []


class Ev:
    __slots__ = ("sem", "val", "eng", "clock")

    def __init__(self, sem, val, eng, clock):
        self.sem = sem
        self.val = val
        self.eng = eng
        self.clock = clock


class Sched:
    def __init__(self, n_dma_sems=8, same_engine_sync=True):
        self.streams = {e: [] for e in ENGS}
        self.cnt = {e: 0 for e in ENGS}
        self.known = {e: {} for e in ENGS}
        self.same_engine_sync = same_engine_sync
        self.n_dma_sems = n_dma_sems
        self.dma_rr = {e: 0 for e in ENGS}
        self.dma_val = {}
        self.dma_last = {}
        self.last_ev = {e: None for e in ENGS}
        self.label = ""
        self.labels = {e: [] for e in ENGS}

    def _wait(self, eng, ev):
        k = self.known[eng]
        if k.get(ev.sem, 0) >= ev.val:
            return
        self.streams[eng].append(("wait", ev.sem, ev.val))
        for s, v in ev.clock.items():
            if k.get(s, 0) < v:
                k[s] = v
        if k.get(ev.sem, 0) < ev.val:
            k[ev.sem] = ev.val

    def _deps(self, eng, reads, writes, extra):
        deps = []
        for t in reads:
            if t.last_w is not None:
                deps.append((t.last_w, "raw"))
        for t in writes:
            if t.last_w is not None:
                deps.append((t.last_w, "waw"))
            for r in t.readers:
                deps.append((r, "war"))
        for ev in extra:
            deps.append((ev, "raw"))
        for ev, kind in deps:
            if ev.eng == eng:
                if eng == "pe":
                    continue
                if not self.same_engine_sync:
                    continue
            self._wait(eng, ev)

    def _finish(self, ev, reads, writes):
        for t in reads:
            t.readers.append(ev)
        for t in writes:
            t.last_w = ev
            t.readers = []

    def op(self, eng, fn, reads=(), writes=(), extra=()):
        self._deps(eng, reads, writes, extra)
        self.cnt[eng] += 1
        sem = "e_" + eng
        val = self.cnt[eng]
        clock = dict(self.known[eng])
        clock[sem] = val
        ev = Ev(sem, val, eng, clock)
        self.streams[eng].append(("op", fn, sem))
        self.labels[eng].append(self.label)
        self._finish(ev, reads, writes)
        self.last_ev[eng] = ev
        return ev

    def dma(self, eng, fn, reads=(), writes=(), extra=(), own_sem=None):
        self._deps(eng, reads, writes, extra)
        if own_sem is not None:
            sem = own_sem
        else:
            i = self.dma_rr[eng]
            self.dma_rr[eng] = (i + 1) % self.n_dma_sems
            sem = "d_%s_%d" % (eng, i)
        prev = self.dma_last.get(sem)
        if prev is not None:
            self._wait(eng, prev)
        val = self.dma_val.get(sem, 0) + 16
        self.dma_val[sem] = val
        clock = dict(self.known[eng])
        ev = Ev(sem, val, "dma_" + eng, clock)
        self.dma_last[sem] = ev
        self.streams[eng].append(("dma", fn, sem))
        self._finish(ev, reads, writes)
        return ev

    def barrier(self, engs=ENGS):
        evs = [self.last_ev[e] for e in ENGS if self.last_ev[e] is not None]
        evs += list(self.dma_last.values())
        for e in engs:
            for ev in evs:
                if ev.eng != e:
                    self._wait(e, ev)

    def sem_keys(self):
        keys = set()
        for e in ENGS:
            for it in self.streams[e]:
                keys.add(it[1] if it[0] == "wait" else it[2])
        return sorted(keys)

    def emit(self, block, sems):
        streams = self.streams

        def replay(name):
            def _f(engobj):
                for it in streams[name]:
                    if it[0] == "wait":
                        engobj.wait_ge(sems[it[1]], it[2])
                    elif it[0] == "op":
                        it[1](engobj).then_inc(sems[it[2]], 1)
                    else:
                        it[1](engobj).then_inc(sems[it[2]], 16)
            return _f

        block.tensor(replay("pe"))
        block.scalar(replay("act"))
        block.vector(replay("dve"))
        block.gpsimd(replay("pool"))
        block.sync(replay("sp"))


class Rot:
    def __init__(self, bufs, name="rot"):
        self.bufs = bufs
        self.ts = [T("%s%d" % (name, i)) for i in range(len(bufs))]
        self.i = 0

    def next(self):
        i = self.i
        self.i = (i + 1) % len(self.bufs)
        return self.bufs[i], self.ts[i]


def MM(out, lhsT, rhs, start, stop):
    return lambda e: e.matmul(out, lhsT, rhs, start=start, stop=stop)


def TR(out, in_, ident):
    return lambda e: e.transpose(out, in_, ident)


def ACTV(out, in_, func, bias=None, scale=None):
    kw = {}
    if bias is not None:
        kw["bias"] = bias
    if scale is not None:
        kw["scale"] = scale
    return lambda e: e.activation(out=out, in_=in_, func=func, **kw)


def TT(out, in0, in1, op):
    return lambda e: e.tensor_tensor(out=out, in0=in0, in1=in1, op=op)


def TS(out, in0, s1, op0, s2=None, op1=None):
    if op1 is None:
        return lambda e: e.tensor_scalar(out=out, in0=in0, scalar1=s1, scalar2=None, op0=op0)
    return lambda e: e.tensor_scalar(out=out, in0=in0, scalar1=s1, scalar2=s2, op0=op0, op1=op1)


def STT(out, in0, scalar, in1, op0, op1):
    return lambda e: e.scalar_tensor_tensor(out=out, in0=in0, scalar=scalar, in1=in1, op0=op0, op1=op1)


def CP(out, in_):
    return lambda e: e.tensor_copy(out=out, in_=in_)


def MSET(ap, v):
    return lambda e: e.memset(ap, v)


def DMA(out, in_):
    return lambda e: e.dma_start(out=out, in_=in_)


QKV_ORDER = [("q", 0), ("k", 0), ("q", 1), ("k", 1), ("q", 2), ("k", 2), ("q", 3), ("k", 3),
             ("q", 4), ("v", 0), ("v", 1), ("q", 5), ("v", 2), ("v", 3), ("q", 6), ("q", 7)]


def stream_entries():
    ents = []
    for m in range(4):
        ents.append(("win%d" % m, 8, 128, "pre", None))
    ents.append(("wkr", 8, 64, "pre64", (32, 48)))
    for fc in range(2):
        ents.append(("wc%d" % fc, 8, 128, "pre", None))
        ents.append(("wx%d" % fc, 8, 128, "pre", None))
        ents.append(("wb%d" % fc, 8, 128, "pre", None))
    for step in QKV_ORDER:
        if step[0] == "q":
            ents.append(("wuq%d" % step[1], 2, 128, "q", (96, 112)))
        elif step[0] == "k":
            ents.append(("wuk%d" % step[1], 2, 128, "kv", None))
        elif step == ("v", 0):
            ents.append(("wuv", 2, 512, "kv512", None))
    for fc in range(2, 4):
        ents.append(("wc%d" % fc, 8, 128, "pre", None))
        ents.append(("wx%d" % fc, 8, 128, "pre", None))
        ents.append(("wb%d" % fc, 8, 128, "pre", None))
    for m in range(8):
        ents.append(("wout%d" % m, 8, 128, "out", None))
    for hf in range(2):
        for jj in range(11):
            ents.append(("wg%d" % (hf * 11 + jj), 8, 128, None, None))
            ents.append(("wu%d" % (hf * 11 + jj), 8, 128, None, None))
        for m in range(8):
            ents.append(("wd%d_%d" % (hf, m), 11, 128, None, None))
    ents.append(("wpp", 2, 1024, None, None))
    for m in range(8):
        ents.append(("wpg%d" % m, 8, 128, None, None))
    return ents


ENTS = stream_entries()
ENT_OFF = {}
_o = 0
for _e in ENTS:
    ENT_OFF[_e[0]] = _o
    _o += _e[1] * _e[2]
TOT = _o
GTYPES = {"pre": (8, 128), "pre64": (8, 64), "q": (2, 128), "kv": (2, 128), "kv512": (2, 512),
          "out": (8, 128), "ffn": (8, 128)}
G_OFF = {}
_o = 0
for _k, (_a, _b) in GTYPES.items():
    G_OFF[_k] = _o
    _o += _a * _b
TOTG = _o
N1 = [e[0] for e in ENTS].index("wg0")

C_IDENT = 0
C_MASK = 128
C_GPOST = 256
C_GFFN = 264
C_GPLE = 272
C_CONVW = 280
C_INVF = 292
C_PHASE = 293
C_GFFNPRE = 294
NCST = 302


def build_nc(nseq=NSEQ, nch=NCH, stop=99):
    nc = bass.Bass("TRN2", target_bir_lowering=False)
    x_d = nc.dram_tensor("x", [NSEQ, S, D], F32, kind="ExternalInput").ap()
    p_d = nc.dram_tensor("p", [NSEQ, S, PLE], F32, kind="ExternalInput").ap()
    pos_d = nc.dram_tensor("pos", [NSEQ, S], I32, kind="ExternalInput").ap()
    wsrc_d = nc.dram_tensor("wsrc", [128, TOT], F32, kind="ExternalInput").ap()
    gsrc_d = nc.dram_tensor("gsrc", [128, TOTG], F32, kind="ExternalInput").ap()
    cst_d = nc.dram_tensor("cst", [128, NCST], F32, kind="ExternalInput").ap()
    out_d = nc.dram_tensor("out", [NSEQ, S, D], F32, kind="ExternalOutput").ap()
    wscr_d = nc.dram_tensor("wscr", [128, TOT], BF16, kind="Internal").ap()

    Sd = Sched()
    es = ExitStack()
    with es:
        def sb(name, shape, dt):
            return es.enter_context(nc.sbuf_tensor(name, shape, dt))

        def ps(name, shape, dt):
            return es.enter_context(nc.psum_tensor(name, shape, dt))

        cst = sb("cst_sb", [128, NCST], F32)
        t_cst = T("cst")
        Sd.dma("sp", DMA(cst[:], cst_d), writes=[t_cst])
        ident = cst[:, C_IDENT:C_IDENT + 128]
        ones_bf = sb("ones_bf", [128, 128], BF16)
        mask_bf = sb("mask_bf", [128, 128], BF16)
        t_ones = T("ones")
        t_mask = T("mask")
        Sd.op("dve", MSET(ones_bf[:], 1.0), writes=[t_ones])
        Sd.op("dve", CP(mask_bf[:], cst[:, C_MASK:C_MASK + 128]), reads=[t_cst], writes=[t_mask])

        t_scr = T("wscr")
        with ExitStack() as pes:
            gt = pes.enter_context(nc.sbuf_tensor("gt", [128, TOTG], F32))
            t_gt = T("gt")
            Sd.dma("sp", DMA(gt[:], gsrc_d), writes=[t_gt])
            NST = 3
            stf = [pes.enter_context(nc.sbuf_tensor("stf%d" % i, [128, 2048], F32)) for i in range(NST)]
            stb = [pes.enter_context(nc.sbuf_tensor("stb%d" % i, [128, 2048], BF16)) for i in range(NST)]
            t_stf = [T("stf%d" % i) for i in range(NST)]
            t_stb = [T("stb%d" % i) for i in range(NST)]
            geng = ["dve", "pool"]
            gi = 0
            for ei, (name, nk, ncols, gtype, neg) in enumerate(ENTS[:N1]):
                sz = nk * ncols
                off = ENT_OFF[name]
                i = ei % NST
                Sd.dma("sp", DMA(stf[i][:, 0:sz], wsrc_d[:, off:off + sz]), writes=[t_stf[i]])
                if gtype is None:
                    Sd.op("act", ACTV(stb[i][:, 0:sz], stf[i][:, 0:sz], AF.Copy),
                          reads=[t_stf[i]], writes=[t_stb[i]])
                else:
                    go = G_OFF[gtype]
                    eng = geng[gi % 2] if sz <= 1024 else "dve"
                    gi += 1
                    Sd.op(eng, TT(stb[i][:, 0:sz], stf[i][:, 0:sz], gt[:, go:go + sz], ALU.mult),
                          reads=[t_stf[i], t_gt], writes=[t_stb[i]])
                if neg is not None:
                    v = stb[i][:, 0:sz].rearrange("p (k c) -> p k c", k=nk)[:, :, neg[0]:neg[1]]
                    Sd.op("act", ACTV(v, v, AF.Copy, scale=-1.0), reads=[t_stb[i]], writes=[t_stb[i]])
                Sd.dma("act", DMA(wscr_d[:, off:off + sz], stb[i][:, 0:sz]), reads=[t_stb[i]], writes=[t_scr])
        Sd.barrier()

        RING = 10240
        ring = sb("ring", [128, RING], BF16)
        xin = [sb("xin%d" % i, [128, D], F32) for i in range(2)]
        xin_rot = Rot(xin, "xin")
        xout_rot = Rot([sb("xout%d" % i, [128, D], F32) for i in range(2)], "xout")
        xTs = [sb("xT%d" % i, [128, 8, CH], F32) for i in range(2)]
        t_xTs = [[T("xT%d_%d" % (j, i)) for i in range(8)] for j in range(2)]
        hT = sb("hT", [128, 8, CH], BF16)
        t_hT = [T("hT%d" % i) for i in range(8)]
        yT = sb("yT", [128, 8, CH], BF16)
        t_yT = [T("yT%d" % i) for i in range(8)]
        sq_rot = Rot([sb("sq%d" % i, [128, CH], BF16) for i in range(3)], "sq")
        rbc_x = sb("rbc_x", [128, CH], F32)
        rbc_q = sb("rbc_q", [128, CH], F32)
        rbc_kv = sb("rbc_kv", [128, CH], F32)
        rbc_o = sb("rbc_o", [128, CH], F32)
        t_rbc_x, t_rbc_q, t_rbc_kv, t_rbc_o = T("rbx"), T("rbq"), T("rbkv"), T("rbo")
        lnt = sb("lnt", [128, CH], F32)
        t_lnt = T("lnt")
        cqT = sb("cqT", [128, 2, CH], BF16)
        ckvT = sb("ckvT", [128, 2, CH], BF16)
        t_cq = [T("cq0"), T("cq1")]
        t_ckv = [T("ckv0"), T("ckv1")]
        uext_rot = Rot([sb("uext%d" % i, [128, CH + 2], F32) for i in range(2)], "uext")
        halo = sb("halo", [128, 4, 2], F32)
        t_halo = [T("halo%d" % i) for i in range(4)]
        tmpf_rot = Rot([sb("tmpf%d" % i, [128, CH], F32) for i in range(5)], "tmpf")
        QT = sb("QT", [128, NH, CH], BF16)
        t_QT = [T("QT%d" % h) for h in range(NH)]
        KT = sb("KT", [128, NH, S], BF16)
        t_KT = [[T("KT%d_%d" % (h, c)) for c in range(NCH)] for h in range(NH)]
        VA = sb("VA", [128, S // 128, 4, 192], BF16)
        t_V = [T("V%d" % c) for c in range(NCH)]
        PT_rot = Rot([sb("PT%d" % i, [128, CH], BF16) for i in range(3)], "PT")
        ROT = sb("ROT_sb", [128, CH], F32)
        t_ROT = T("ROT")
        posi = sb("posi", [128, CH], I32)
        t_posi = T("posi")
        actb = sb("actb", [128, 11, CH], BF16)
        t_act = [T("act%d" % i) for i in range(11)]
        sil_rot = Rot([sb("sil%d" % i, [128, CH], BF16) for i in range(2)], "sil")
        pin_rot = Rot([sb("pin%d" % i, [128, PLE], F32) for i in range(4)], "pin")
        pT = sb("pT", [128, 2, CH], BF16)
        t_pT = T("pT")
        t_out = T("out")

        PS = [ps("ps%d" % i, [128, 512], F32) for i in range(8)]
        ps_rot = Rot(PS[0:5], "psr")
        pso_rot = Rot(PS[5:7], "pso")
        PSN = PS[7]
        t_PSN = T("psn")

        Sd.op("pool", MSET(QT[96:128, :, :], 0.0), writes=t_QT)
        Sd.op("pool", MSET(KT[96:128, :, :], 0.0), writes=[t for row in t_KT for t in row])
        for c in range(NCH):
            Sd.op("pool", MSET(VA[:, c * 4:(c + 1) * 4, :, 64:128], 1.0), writes=[t_V[c]])

        p2 = list(ENTS[N1:])
        t_scr_e = {e[0]: T("scr_" + e[0]) for e in p2}
        p2_stored = set()
        p2_k = [0]

        def pump(n):
            lab = Sd.label
            Sd.label = "prep2"
            for _ in range(n):
                k = p2_k[0]
                if k >= len(p2):
                    break
                name, nk, ncols, gtype, neg = p2[k]
                sz, off = nk * ncols, ENT_OFF[name]
                Sd.dma("pool", DMA(wscr_d[:, off:off + sz], wsrc_d[:, off:off + sz]), writes=[t_scr_e[name]],
                       own_sem="d2_%d" % k)
                p2_stored.add(name)
                p2_k[0] += 1
            Sd.label = lab

        class Ring:
            def __init__(self):
                self.live = []
                self.head = 0
                self.issued = 0
                self.got = 0
                self.slots = {}

            def _try_issue(self):
                idx = self.issued
                name, nk, ncols, _, _ = ENTS[idx % len(ENTS)]
                if name in t_scr_e and name not in p2_stored:
                    return False
                sz = nk * ncols
                start = self.head
                if start + sz > RING:
                    start = 0
                end = start + sz
                ov = [l for l in self.live if not (l[1] <= start or l[0] >= end)]
                for l in ov:
                    if l[3] >= self.got - 1:
                        return False
                extra = []
                for l in ov:
                    extra += l[2].readers
                    if l[2].last_w is not None:
                        extra.append(l[2].last_w)
                    self.live.remove(l)
                t = T("w%d" % idx)
                off = ENT_OFF[name]
                Sd.dma("sp", DMA(ring[:, start:end], wscr_d[:, off:off + sz]), reads=[t_scr_e.get(name, t_scr)],
                       writes=[t], extra=extra)
                self.live.append((start, end, t, idx))
                self.slots[idx] = (start, t, nk, ncols)
                self.head = end
                self.issued += 1
                return True

            def fill(self, total):
                while self.issued < total and self.issued < self.got + 40:
                    if not self._try_issue():
                        break

            def get(self, name, total):
                idx = self.got
                assert ENTS[idx % len(ENTS)][0] == name, (ENTS[idx % len(ENTS)][0], name)
                self.fill(total)
                assert idx in self.slots, "ring too small for " + name
                start, t, nk, ncols = self.slots.pop(idx)
                self.got += 1
                v = ring[:, start:start + nk * ncols].rearrange("p (k c) -> p k c", k=nk)
                return v, t

        R = Ring()
        TOTAL_ENT = nseq * nch * len(ENTS)

        def getw(name):
            return R.get(name, TOTAL_ENT)

        def sumsq_bcast(srcs, dst_rbc, t_dst, n, extra_scale=1.0, keep_sq=None, sq_eng="act"):
            nsrc = len(srcs)
            for i, (ap, ts) in enumerate(srcs):
                if keep_sq is not None:
                    sqb, t_sq = keep_sq[i]
                else:
                    sqb, t_sq = sq_rot.next()
                    sqb = sqb[:]
                if sq_eng == "act":
                    Sd.op("act", ACTV(sqb, ap, AF.Square), reads=ts, writes=[t_sq])
                else:
                    Sd.op("dve", TT(sqb, ap, ap, ALU.mult), reads=ts, writes=[t_sq])
                Sd.op("pe", MM(PSN[:], ones_bf[:], sqb, i == 0, i == nsrc - 1), reads=[t_sq, t_ones],
                      writes=[t_PSN])
            Sd.op("act", ACTV(lnt[:], PSN[:], AF.Ln, bias=EPS, scale=1.0 / n), reads=[t_PSN], writes=[t_lnt])
            Sd.op("act", ACTV(dst_rbc, lnt[:], AF.Exp, scale=-0.5),
                  reads=[t_lnt], writes=[t_dst])

        evac_flip = [0]

        def evac(out, in_, reads, writes):
            evac_flip[0] ^= 1
            if evac_flip[0]:
                Sd.op("act", ACTV(out, in_, AF.Copy), reads=reads, writes=writes)
            else:
                Sd.op("dve", CP(out, in_), reads=reads, writes=writes)

        xpre = {}

        def eng8(i):
            return "dve"

        def rope_table(s, c):
            Sd.label = "rope"
            tok0 = c * CH
            Sd.dma("sp", DMA(posi[:], pos_d[s, tok0:tok0 + CH].partition_broadcast(128)), writes=[t_posi])
            a0, ta0 = tmpf_rot.next()
            a1, ta1 = tmpf_rot.next()
            a2, ta2 = tmpf_rot.next()
            Sd.op("pool", CP(a0[:], posi[:]), reads=[t_posi], writes=[ta0])
            Sd.op("pool", TS(a0[:], a0[:], cst[:, C_INVF:C_INVF + 1], ALU.mult,
                             cst[:, C_PHASE:C_PHASE + 1], ALU.add), reads=[ta0, t_cst], writes=[ta0])
            Sd.op("pool", TS(a1[:], a0[:], 1.0 / (2 * math.pi), ALU.mult, 0.0, ALU.add), reads=[ta0], writes=[ta1])
            Sd.op("pool", CP(posi[:], a1[:]), reads=[ta1], writes=[t_posi])
            Sd.op("pool", CP(a1[:], posi[:]), reads=[t_posi], writes=[ta1])
            Sd.op("dve", STT(a0[:], a1[:], -2 * math.pi, a0[:], ALU.mult, ALU.add), reads=[ta0, ta1], writes=[ta0])
            Sd.op("dve", TS(a2[:], a0[:], math.pi, ALU.is_gt, -2 * math.pi, ALU.mult), reads=[ta0], writes=[ta2])
            Sd.op("dve", TT(a0[:], a0[:], a2[:], ALU.add), reads=[ta0, ta2], writes=[ta0])
            Sd.op("dve", TS(a2[:], a0[:], -math.pi, ALU.is_lt, 2 * math.pi, ALU.mult), reads=[ta0], writes=[ta2])
            Sd.op("dve", TT(a0[:], a0[:], a2[:], ALU.add), reads=[ta0, ta2], writes=[ta0])
            Sd.op("act", ACTV(ROT[:], a0[:], AF.Sin), reads=[ta0], writes=[t_ROT])

        def x_tiles(s, c, par, tts, act_only=False):
            Sd.label = "xload"
            for tt in tts:
                if (s, c, tt) in xpre:
                    xb, t_xb = xpre.pop((s, c, tt))
                else:
                    xb, t_xb = xin_rot.next()
                    Sd.dma("sp", DMA(xb[:], x_d[s, c * CH + tt * 128:c * CH + (tt + 1) * 128, :]), writes=[t_xb])
                for half in range(2):
                    pb, t_pb = ps_rot.next()
                    for q4 in range(4):
                        fc = half * 4 + q4
                        Sd.op("pe", TR(pb[:, q4 * 128:(q4 + 1) * 128], xb[:, fc * 128:(fc + 1) * 128], ident),
                              reads=[t_xb, t_cst], writes=[t_pb])
                    if act_only:
                        Sd.op("act", ACTV(xTs[par][:, half * 4:half * 4 + 4, tt * 128:(tt + 1) * 128],
                                          pb[:, :].rearrange("p (a b) -> p a b", a=4), AF.Copy), reads=[t_pb],
                              writes=t_xTs[par][half * 4:half * 4 + 4])
                    else:
                        evac(xTs[par][:, half * 4:half * 4 + 4, tt * 128:(tt + 1) * 128],
                             pb[:, :].rearrange("p (a b) -> p a b", a=4), [t_pb], t_xTs[par][half * 4:half * 4 + 4])

        def x_prefetch(s, c, tts):
            for tt in tts:
                xb, t_xb = xin_rot.next()
                Sd.dma("sp", DMA(xb[:], x_d[s, c * CH + tt * 128:c * CH + (tt + 1) * 128, :]), writes=[t_xb])
                xpre[(s, c, tt)] = (xb, t_xb)

        def chunk_after(s, c):
            return (s, c + 1) if c + 1 < nch else ((s + 1, 0) if s + 1 < nseq else None)

        def front_A(s, c, par, act_only=False):
            x_tiles(s, c, par, (0, 1), act_only)
            x_prefetch(s, c, (2, 3))

        def front_B(s, c, par, act_only=False):
            x_tiles(s, c, par, (2, 3), act_only)
            n2 = chunk_after(s, c)
            if n2 is not None:
                x_prefetch(n2[0], n2[1], (0, 1))
            Sd.label = "prenorm"
            sumsq_bcast([(xTs[par][:, fc, :], [t_xTs[par][fc]]) for fc in range(8)], rbc_kv[:], t_rbc_kv, D)

        front_A(0, 0, 0)
        front_B(0, 0, 0)
        rope_table(0, 0)
        h_ready = [False]
        for s in range(nseq):
            for c in range(nch):
                tok0 = c * CH
                first_chunk = (s == 0 and c == 0)

                def P(n):
                    if first_chunk:
                        pump(n)

                par = (s * nch + c) % 2
                xT = xTs[par]
                t_xT = t_xTs[par]
                nxt = chunk_after(s, c)
                pins = []
                for tt in range(4):
                    pbuf, t_pbuf = pin_rot.next()
                    Sd.dma("act", DMA(pbuf[:], p_d[s, tok0 + tt * 128:tok0 + (tt + 1) * 128, :]), writes=[t_pbuf])
                    pins.append((pbuf, t_pbuf))
                if stop <= 2:
                    continue
                Sd.label = "prenorm"
                if not h_ready[0]:
                    for fc in range(8):
                        Sd.op("dve", TT(hT[:, fc, :], xT[:, fc, :], rbc_kv[:], ALU.mult),
                              reads=[t_xT[fc], t_rbc_kv], writes=[t_hT[fc]])
                h_ready[0] = False

                if stop <= 3:
                    continue
                Sd.label = "latents"
                def proj8(wname, rhs_t=hT, rhs_ts=t_hT, M=128):
                    w, t_w = getw(wname)
                    pb, t_pb = ps_rot.next()
                    for kc in range(8):
                        Sd.op("pe", MM(pb[0:M, :], w[:, kc, :], rhs_t[:, kc, :], kc == 0, kc == 7),
                              reads=[t_w, rhs_ts[kc]], writes=[t_pb])
                    return pb, t_pb

                raws = []
                for i in range(2):
                    pb, t_pb = proj8("win%d" % i)
                    rw, t_rw = tmpf_rot.next()
                    Sd.op("act", ACTV(rw[:], pb[:], AF.Copy), reads=[t_pb], writes=[t_rw])
                    sqb, t_sq = sq_rot.next()
                    Sd.op("act", ACTV(sqb[:], pb[:], AF.Square), reads=[t_pb], writes=[t_sq])
                    Sd.op("pe", MM(PSN[:], ones_bf[:], sqb[:], i == 0, i == 1), reads=[t_sq, t_ones], writes=[t_PSN])
                    raws.append((rw, t_rw))
                Sd.op("act", ACTV(lnt[:], PSN[:], AF.Ln, bias=EPS, scale=1.0 / 256), reads=[t_PSN], writes=[t_lnt])
                Sd.op("act", ACTV(rbc_q[:], lnt[:], AF.Exp, scale=-0.5), reads=[t_lnt], writes=[t_rbc_q])
                for i in range(2):
                    rw, t_rw = raws[i]
                    Sd.op("pool", TT(cqT[:, i, :], rw[:], rbc_q[:], ALU.mult),
                          reads=[t_rw, t_rbc_q], writes=[t_cq[i]])
                raws = []
                for i in range(2):
                    pb, t_pb = proj8("win%d" % (2 + i))
                    rw, t_rw = tmpf_rot.next()
                    Sd.op("act", ACTV(rw[:], pb[:], AF.Copy), reads=[t_pb], writes=[t_rw])
                    sqb, t_sq = sq_rot.next()
                    Sd.op("act", ACTV(sqb[:], pb[:], AF.Square), reads=[t_pb], writes=[t_sq])
                    Sd.op("pe", MM(PSN[:], ones_bf[:], sqb[:], i == 0, i == 1), reads=[t_sq, t_ones], writes=[t_PSN])
                    raws.append((rw, t_rw))
                Sd.op("act", ACTV(lnt[:], PSN[:], AF.Ln, bias=EPS, scale=1.0 / 256), reads=[t_PSN], writes=[t_lnt])
                Sd.op("act", ACTV(rbc_kv[:], lnt[:], AF.Exp, scale=-0.5), reads=[t_lnt], writes=[t_rbc_kv])
                for i in range(2):
                    rw, t_rw = raws[i]
                    Sd.op("pool", TT(ckvT[:, i, :], rw[:], rbc_kv[:], ALU.mult),
                          reads=[t_rw, t_rbc_kv], writes=[t_ckv[i]])

                P(6)
                if stop <= 4:
                    continue
                Sd.label = "krope"
                pb, t_pb = proj8("wkr", M=64)
                t1, tt1 = tmpf_rot.next()
                t2, tt2 = tmpf_rot.next()
                Sd.op("dve", TT(t1[64:96, :], pb[0:32, :], ROT[0:32, :], ALU.mult), reads=[t_pb, t_ROT], writes=[tt1])
                Sd.op("dve", TT(t2[64:96, :], pb[32:64, :], ROT[32:64, :], ALU.mult), reads=[t_pb, t_ROT], writes=[tt2])
                for h in range(NH):
                    Sd.op("pool", TT(KT[64:96, h, tok0:tok0 + CH], t1[64:96, :], t2[64:96, :], ALU.add),
                          reads=[tt1, tt2], writes=[t_KT[h][c]])

                if stop <= 6:
                    continue
                def conv_fc(fc):
                    Sd.label = "conv"
                    pc, t_pc = proj8("wc%d" % fc)
                    px, t_px = proj8("wx%d" % fc)
                    pbb, t_pbb = proj8("wb%d" % fc)
                    csb, t_csb = tmpf_rot.next()
                    acc, t_acc = tmpf_rot.next()
                    Sd.op("act", ACTV(csb[:], pc[:], AF.Copy), reads=[t_pc], writes=[t_csb])
                    ue, t_ue = uext_rot.next()
                    if c == 0:
                        Sd.op("pool", MSET(ue[:, 0:2], 0.0), writes=[t_ue])
                    else:
                        Sd.op("pool", CP(ue[:, 0:2], halo[:, fc, :]), reads=[t_halo[fc]], writes=[t_ue])
                    Sd.op("dve", TT(ue[:, 2:CH + 2], px[:], csb[:], ALU.mult), reads=[t_px, t_csb],
                          writes=[t_ue])
                    Sd.op("pool", CP(halo[:, fc, :], ue[:, CH:CH + 2]), reads=[t_ue], writes=[t_halo[fc]])
                    cw = lambda k: cst[:, C_CONVW + k * 4 + fc:C_CONVW + k * 4 + fc + 1]
                    Sd.op("pool", TS(acc[:], ue[:, 2:CH + 2], cw(0), ALU.mult, 0.0, ALU.add), reads=[t_ue, t_cst],
                          writes=[t_acc])
                    Sd.op("dve", STT(acc[:], ue[:, 1:CH + 1], cw(1), acc[:], ALU.mult, ALU.add),
                          reads=[t_ue, t_acc, t_cst], writes=[t_acc])
                    Sd.op("dve", STT(acc[:], ue[:, 0:CH], cw(2), acc[:], ALU.mult, ALU.add),
                          reads=[t_ue, t_acc, t_cst], writes=[t_acc])
                    Sd.op("dve", TT(yT[:, 4 + fc, :], pbb[:], acc[:], ALU.mult), reads=[t_pbb, t_acc],
                          writes=[t_yT[4 + fc]])
                    Sd.label = "attn"

                conv_fc(0)
                conv_fc(1)
                P(6)
                def q_head(h):
                    Sd.label = "qproj"
                    w, t_w = getw("wuq%d" % h)
                    pb, t_pb = ps_rot.next()
                    for kc in range(2):
                        Sd.op("pe", MM(pb[:], w[:, kc, :], cqT[:, kc, :], kc == 0, kc == 1),
                              reads=[t_w, t_cq[kc]], writes=[t_pb])
                    t1, tt1 = tmpf_rot.next()
                    t2, tt2 = tmpf_rot.next()
                    Sd.op("dve", TT(t1[64:96, :], pb[64:96, :], ROT[64:96, :], ALU.mult), reads=[t_pb, t_ROT],
                          writes=[tt1])
                    ev2 = Sd.op("dve", TT(t2[64:96, :], pb[96:128, :], ROT[96:128, :], ALU.mult), reads=[t_pb, t_ROT],
                                writes=[tt2])
                    Sd.op("act", ACTV(QT[0:64, h, :], pb[0:64, :], AF.Copy), reads=[t_pb], writes=[t_QT[h]], extra=[ev2])
                    Sd.op("pool", TT(QT[64:96, h, :], t1[64:96, :], t2[64:96, :], ALU.add), reads=[tt1, tt2],
                          writes=[t_QT[h]])

                def k_pair(hp):
                    Sd.label = "kproj"
                    w, t_w = getw("wuk%d" % hp)
                    pb, t_pb = ps_rot.next()
                    for kc in range(2):
                        Sd.op("pe", MM(pb[:], w[:, kc, :], ckvT[:, kc, :], kc == 0, kc == 1),
                              reads=[t_w, t_ckv[kc]], writes=[t_pb])
                    Sd.op("act", ACTV(KT[0:64, 2 * hp, tok0:tok0 + CH], pb[0:64, :], AF.Copy),
                          reads=[t_pb], writes=[t_KT[2 * hp][c]])
                    Sd.op("act", ACTV(KT[0:64, 2 * hp + 1, tok0:tok0 + CH], pb[64:128, :], AF.Copy),
                          reads=[t_pb], writes=[t_KT[2 * hp + 1][c]])

                wv_ = [None]

                def v_tile(tt):
                    Sd.label = "vproj"
                    if wv_[0] is None:
                        wv_[0] = getw("wuv")
                    w, t_w = wv_[0]
                    pb, t_pb = ps_rot.next()
                    for kc in range(2):
                        Sd.op("pe", MM(pb[:], ckvT[:, kc, tt * 128:(tt + 1) * 128], w[:, kc, :], kc == 0, kc == 1),
                              reads=[t_w, t_ckv[kc]], writes=[t_pb])
                    pv4 = pb[:, :].rearrange("p (h b d) -> p h b d", h=4, b=2)
                    for b_ in range(2):
                        Sd.op("act", ACTV(VA[:, c * 4 + tt, :, b_ * 128:b_ * 128 + 64], pv4[:, :, b_, :], AF.Copy),
                              reads=[t_pb], writes=[t_V[c]])

                for step in QKV_ORDER:
                    if step[0] == "q":
                        q_head(step[1])
                    elif step[0] == "k":
                        k_pair(step[1])
                    else:
                        v_tile(step[1])
                P(12)
                P(3)
                if stop <= 9:
                    continue
                Sd.label = "attn"
                nkt = 4 * c + 4
                LA = 2
                seq = [(h, kt) for h in range(NH) for kt in range(nkt)]
                pos = {}

                def score(h, kt):
                    j = kt - 4 * c
                    q0 = max(j, 0) * 128
                    pb, t_pb = ps_rot.next()
                    Sd.op("pe", MM(pb[:, q0:CH], KT[:, h, kt * 128:(kt + 1) * 128], QT[:, h, q0:CH],
                                   True, True), reads=[t_KT[h][kt // 4], t_QT[h]], writes=[t_pb])
                    return pb, t_pb, q0, j

                def expo(sc):
                    pb, t_pb, q0, j = sc
                    ptb, t_pt = PT_rot.next()
                    Sd.op("act", ACTV(ptb[:, q0:CH], pb[:, q0:CH], AF.Exp, scale=QSCALE), reads=[t_pb], writes=[t_pt])
                    if j >= 0:
                        Sd.op("pool", TT(ptb[:, q0:q0 + 128], ptb[:, q0:q0 + 128], mask_bf[:], ALU.mult),
                              reads=[t_pt, t_mask], writes=[t_pt])
                    return ptb, t_pt, q0

                def pv(h, kt, pt):
                    ptb, t_pt, q0 = pt
                    po, t_po = pos[h]
                    Sd.op("pe", MM(po[:, q0:CH], VA[:, kt, h // 2, (h % 2) * 64:(h % 2) * 64 + 128], ptb[:, q0:CH],
                                   kt == 0, kt == nkt - 1),
                          reads=[t_pt, t_V[kt // 4]], writes=[t_po])

                def epilogue(h):
                    po, t_po = pos.pop(h)
                    l1, tl1 = tmpf_rot.next()
                    hp_, ho = h // 2, (h % 2) * 64
                    Sd.op("act", ACTV(l1[ho:ho + 64, :], po[64 - ho:128 - ho, :], AF.Ln), reads=[t_po], writes=[tl1])
                    Sd.op("act", ACTV(l1[ho:ho + 64, :], l1[ho:ho + 64, :], AF.Exp, scale=-1.0), reads=[tl1], writes=[tl1])
                    Sd.op("dve", TT(yT[ho:ho + 64, hp_, :], po[ho:ho + 64, :], l1[ho:ho + 64, :], ALU.mult),
                          reads=[t_po, tl1], writes=[t_yT[hp_]])
                    P(6)
                    if h in (0, 2):
                        conv_fc(h // 2 + 2)
                        if h == 2:
                            Sd.label = "gnorm"
                            sumsq_bcast([(yT[:, 4 + i, :], [t_yT[4 + i]]) for i in range(4)], rbc_o[:], t_rbc_o, 512,
                                        sq_eng="dve")
                            for i in range(4):
                                Sd.op("dve" if i % 2 == 0 else "pool", TT(hT[:, 4 + i, :], yT[:, 4 + i, :], rbc_o[:], ALU.mult),
                                      reads=[t_yT[4 + i], t_rbc_o], writes=[t_hT[4 + i]])
                            Sd.label = "attn"

                nseqq = len(seq)
                conv_iters = [(hh + 1) * nkt for hh in (0, 2)]

                def target(i):
                    t = i + 3
                    for ci in conv_iters:
                        if ci >= i + 1 or ci == i:
                            pass
                    nxt_ci = [ci for ci in conv_iters if ci > i]
                    if nxt_ci:
                        t = min(t, nxt_ci[0] + 1)
                    if i in conv_iters:
                        t = min(t, i + 2)
                    return min(t, nseqq - 1)

                scs = {}
                issued = 0
                t0_ = min(2, nseqq - 1, conv_iters[0] + 1)
                while issued <= t0_:
                    scs[issued] = score(*seq[issued])
                    issued += 1
                pend_ep = None
                for i in range(nseqq):
                    h, kt = seq[i]
                    if kt == 0:
                        pos[h] = pso_rot.next()
                    pt = expo(scs.pop(i))
                    if pend_ep is not None:
                        epilogue(pend_ep)
                        pend_ep = None
                    tg = target(i)
                    while issued <= tg:
                        scs[issued] = score(*seq[issued])
                        issued += 1
                    pv(h, kt, pt)
                    if kt == nkt - 1:
                        pend_ep = h
                epilogue(pend_ep)

                if first_chunk:
                    pump(len(p2) + 2)
                if stop <= 10:
                    continue
                Sd.label = "gnorm"
                sumsq_bcast([(yT[:, i, :], [t_yT[i]]) for i in range(4)], rbc_o[:], t_rbc_o, 512)
                for i in range(4):
                    Sd.op("dve" if i != 3 else "pool", TT(hT[:, i, :], yT[:, i, :], rbc_o[:], ALU.mult),
                          reads=[t_yT[i], t_rbc_o], writes=[t_hT[i]])

                if stop <= 11:
                    continue
                Sd.label = "wout"
                def sq_and_sum(src_ap, src_ts, acc_ps, t_acc, first, last):
                    sqb, t_sq = sq_rot.next()
                    Sd.op("act", ACTV(sqb[:], src_ap, AF.Square), reads=src_ts, writes=[t_sq])
                    return lambda: Sd.op("pe", MM(acc_ps[:], ones_bf[:], sqb[:], first, last), reads=[t_sq, t_ones],
                                         writes=[t_acc])

                def rstd_from(acc_ps, t_acc, dst, t_dst, n):
                    Sd.op("act", ACTV(lnt[:], acc_ps[:], AF.Ln, bias=EPS, scale=1.0 / n), reads=[t_acc], writes=[t_lnt])
                    Sd.op("act", ACTV(dst[:], lnt[:], AF.Exp, scale=-0.5), reads=[t_lnt], writes=[t_dst])

                def residual_add(next_norm):
                    pend = None
                    for m in range(8):
                        e_ = eng8(m)
                        tb, t_tb = tmpf_rot.next()
                        Sd.op(e_, TT(tb[:], yT[:, m, :], rbc_o[:], ALU.mult), reads=[t_yT[m], t_rbc_o], writes=[t_tb])
                        Sd.op(e_, TT(xT[:, m, :], xT[:, m, :], tb[:], ALU.add), reads=[t_tb, t_xT[m]], writes=[t_xT[m]])
                        if next_norm:
                            nxt_ = sq_and_sum(xT[:, m, :], [t_xT[m]], PSN, t_PSN, m == 0, m == 7)
                            if pend is not None:
                                pend()
                            pend = nxt_
                    if pend is not None:
                        pend()

                def gcol(base, m):
                    return cst[:, base + m:base + m + 1]

                pend = None
                for m in range(8):
                    pb, t_pb = proj8("wout%d" % m)
                    if pend is not None:
                        pend()
                    pend = sq_and_sum(pb[:], [t_pb], PSN, t_PSN, m == 0, m == 7)
                    Sd.op("act", ACTV(yT[:, m, :], pb[:], AF.Identity, scale=gcol(C_GPOST, m)), reads=[t_pb, t_cst],
                          writes=[t_yT[m]])
                pend()
                rstd_from(PSN, t_PSN, rbc_o, t_rbc_o, D)
                if nxt is not None:
                    front_A(nxt[0], nxt[1], 1 - par, act_only=True)
                    Sd.label = "wout"
                residual_add(True)

                if stop <= 12:
                    continue
                Sd.label = "ffn"
                def xg_op(fc):
                    xg, t_xg = tmpf_rot.next()
                    Sd.op("act", ACTV(xg[:], xT[:, fc, :], AF.Identity, scale=gcol(C_GFFNPRE, fc)),
                          reads=[t_xT[fc], t_cst], writes=[t_xg])
                    return xg, t_xg
                xgs = [xg_op(fc) for fc in range(3)]
                rstd_from(PSN, t_PSN, rbc_x, t_rbc_x, D)
                for fc in range(8):
                    xg, t_xg = xgs[fc]
                    Sd.op("dve", TT(hT[:, fc, :], xg[:], rbc_x[:], ALU.mult),
                          reads=[t_xg, t_rbc_x], writes=[t_hT[fc]])
                    if fc + 3 < 8:
                        xgs.append(xg_op(fc + 3))
                ffn_pend = []
                for hf in range(2):
                    for jj in range(11):
                        j = hf * 11 + jj
                        pg, t_pg = proj8("wg%d" % j)
                        pu, t_pu = proj8("wu%d" % j)
                        sl, t_sl = sil_rot.next()
                        Sd.op("act", ACTV(sl[:], pg[:], AF.Silu), reads=[t_pg], writes=[t_sl])
                        Sd.op("dve", TT(actb[:, jj, :], pu[:], sl[:], ALU.mult), reads=[t_pu, t_sl], writes=[t_act[jj]])
                    for m in range(8):
                        w, t_w = getw("wd%d_%d" % (hf, m))
                        pb, t_pb = ps_rot.next()
                        for jj in range(11):
                            Sd.op("pe", MM(pb[:], w[:, jj, :], actb[:, jj, :], jj == 0, jj == 10),
                                  reads=[t_w, t_act[jj]], writes=[t_pb])
                        if hf == 0:
                            Sd.op("act", ACTV(yT[:, m, :], pb[:], AF.Copy), reads=[t_pb], writes=[t_yT[m]])
                        else:
                            Sd.op("dve", TT(yT[:, m, :], pb[:], yT[:, m, :], ALU.add), reads=[t_pb, t_yT[m]],
                                  writes=[t_yT[m]])
                            nxt_ = sq_and_sum(yT[:, m, :], [t_yT[m]], PSN, t_PSN, m == 0, m == 7)
                            for f_ in ffn_pend:
                                f_()
                            ffn_pend = [nxt_]
                            Sd.op("act", ACTV(yT[:, m, :], yT[:, m, :], AF.Identity, scale=gcol(C_GFFN, m)),
                                  reads=[t_yT[m], t_cst], writes=[t_yT[m]])
                    if hf == 0 and nxt is not None:
                        rope_table(nxt[0], nxt[1])
                        Sd.label = "ffn"

                if stop <= 13:
                    continue
                Sd.label = "ple"
                for tt in range(4):
                    pbuf, t_pbuf = pins[tt]
                    pb, t_pb = ps_rot.next()
                    for i in range(2):
                        Sd.op("pe", TR(pb[:, i * 128:(i + 1) * 128], pbuf[:, i * 128:(i + 1) * 128], ident),
                              reads=[t_pbuf, t_cst], writes=[t_pb])
                    Sd.op("dve", CP(pT[:, :, tt * 128:(tt + 1) * 128], pb[:, 0:256].rearrange("p (a b) -> p a b", a=2)),
                          reads=[t_pb], writes=[t_pT])
                Sd.label = "ffn"
                for f_ in ffn_pend:
                    f_()
                rstd_from(PSN, t_PSN, rbc_o, t_rbc_o, D)
                Sd.label = "ple"
                w, t_w = getw("wpp")
                PSE, t_PSE = pso_rot.next()
                pend = None
                for m in range(8):
                    pb, t_pb = ps_rot.next()
                    for kc in range(2):
                        Sd.op("pe", MM(pb[:], w[:, kc, m * 128:(m + 1) * 128], pT[:, kc, :], kc == 0, kc == 1),
                              reads=[t_w, t_pT], writes=[t_pb])
                    if pend is not None:
                        pend()
                    pend = sq_and_sum(pb[:], [t_pb], PSE, t_PSE, m == 0, m == 7)
                    Sd.op("act", ACTV(actb[:, m, :], pb[:], AF.Identity, scale=gcol(C_GPLE, m)), reads=[t_pb, t_cst],
                          writes=[t_act[m]])
                pend()
                Sd.label = "ffn"
                if nxt is not None:
                    front_B(nxt[0], nxt[1], 1 - par, act_only=True)
                    Sd.label = "ffn"
                residual_add(False)
                Sd.label = "ple"
                rstd_from(PSE, t_PSE, rbc_q, t_rbc_q, D)
                for m in range(8):
                    Sd.op(eng8(m), TT(actb[:, m, :], actb[:, m, :], rbc_q[:], ALU.mult), reads=[t_act[m], t_rbc_q],
                          writes=[t_act[m]])
                for fc in range(8):
                    if fc % 2 == 0:
                        Sd.op("act", ACTV(hT[:, fc, :], xT[:, fc, :], AF.Copy), reads=[t_xT[fc]], writes=[t_hT[fc]])
                    else:
                        Sd.op(eng8(fc), CP(hT[:, fc, :], xT[:, fc, :]), reads=[t_xT[fc]], writes=[t_hT[fc]])
                for m in range(8):
                    pb, t_pb = proj8("wpg%d" % m)
                    sg, t_sg = tmpf_rot.next()
                    Sd.op("act", ACTV(sg[:], pb[:], AF.Sigmoid), reads=[t_pb], writes=[t_sg])
                    Sd.op("dve", TT(sg[:], sg[:], actb[:, m, :], ALU.mult), reads=[t_sg, t_act[m]], writes=[t_sg])
                    Sd.op(eng8(m), TT(xT[:, m, :], xT[:, m, :], sg[:], ALU.add), reads=[t_sg, t_xT[m]], writes=[t_xT[m]])

                if stop <= 14:
                    continue
                if nxt is not None:
                    Sd.label = "prenorm"
                    for fc in range(8):
                        Sd.op("dve", TT(hT[:, fc, :], xTs[1 - par][:, fc, :], rbc_kv[:], ALU.mult),
                              reads=[t_xTs[1 - par][fc], t_rbc_kv], writes=[t_hT[fc]])
                    h_ready[0] = True
                Sd.label = "store"
                for tt in range(4):
                    xo, t_xo = xout_rot.next()
                    for half in range(2):
                        pb, t_pb = ps_rot.next()
                        for q4 in range(4):
                            fc = half * 4 + q4
                            Sd.op("pe", TR(pb[:, q4 * 128:(q4 + 1) * 128], xT[:, fc, tt * 128:(tt + 1) * 128], ident),
                                  reads=[t_xT[fc], t_cst], writes=[t_pb])
                        if tt < 2:
                            Sd.op("act", ACTV(xo[:, half * 512:(half + 1) * 512], pb[:], AF.Copy), reads=[t_pb],
                                  writes=[t_xo])
                        else:
                            Sd.op("dve", CP(xo[:, half * 512:(half + 1) * 512], pb[:]), reads=[t_pb], writes=[t_xo])
                    Sd.dma("pool", DMA(out_d[s, tok0 + tt * 128:tok0 + (tt + 1) * 128, :], xo[:]), reads=[t_xo],
                           writes=[t_out])

        for ev in Sd.dma_last.values():
            if ev.eng == "dma_pool":
                Sd._wait("pool", ev)

        nc._sched_labels = Sd.labels
        keys = Sd.sem_keys()
        sems = {k: es.enter_context(nc.semaphore(k)) for k in keys}
        block = es.enter_context(nc.Block())
        Sd.emit(block, sems)
    return nc


def _ent_k(W, cols, nk):
    W = np.asarray(W)
    return np.ascontiguousarray(W.reshape(nk, 128, W.shape[1])[:, :, cols].transpose(1, 0, 2)).reshape(128, -1)


def _gain_k(g, nk, ncols):
    g = np.asarray(g, dtype=np.float32).reshape(nk, 128)
    return np.ascontiguousarray(np.broadcast_to(g.T[:, :, None], (128, nk, ncols))).reshape(128, -1)


def host_layout(inp):
    w_in = inp["w_in"][0]
    w_uq = inp["w_uq"][0]
    w_ukv = inp["w_ukv"][0]
    w_out = inp["w_out"][0]
    w_gate = inp["w_gate"][0]
    w_up = inp["w_up"][0]
    w_down = inp["w_down"][0]
    w_pp = inp["w_ple_proj"][0]
    w_pg = inp["w_ple_gate"][0]
    ar = np.arange
    parts = {}
    for m in range(2):
        parts["win%d" % m] = _ent_k(w_in, ar(m * 128, (m + 1) * 128), 8)
        parts["win%d" % (2 + m)] = _ent_k(w_in, O1 + ar(m * 128, (m + 1) * 128), 8)
    krc = np.concatenate([O2 + ar(0, 32), O2 + ar(16, 32), O2 + ar(0, 16)])
    parts["wkr"] = _ent_k(w_in, krc, 8)
    for fc in range(4):
        parts["wb%d" % fc] = _ent_k(w_in, O3 + ar(fc * 128, (fc + 1) * 128), 8)
        parts["wc%d" % fc] = _ent_k(w_in, O4 + ar(fc * 128, (fc + 1) * 128), 8)
        parts["wx%d" % fc] = _ent_k(w_in, O5 + ar(fc * 128, (fc + 1) * 128), 8)
    for h in range(NH):
        b = h * 96
        qc = np.concatenate([b + ar(0, 96), b + ar(80, 96), b + ar(64, 80)])
        parts["wuq%d" % h] = _ent_k(w_uq, qc, 2)
    for hp in range(4):
        kc = np.concatenate([(2 * hp) * 128 + ar(0, 64), (2 * hp + 1) * 128 + ar(0, 64)])
        parts["wuk%d" % hp] = _ent_k(w_ukv, kc, 2)
    vc = np.concatenate([h * 128 + 64 + ar(0, 64) for h in range(NH)])
    parts["wuv"] = _ent_k(w_ukv, vc, 2)
    for m in range(8):
        parts["wout%d" % m] = _ent_k(w_out, ar(m * 128, (m + 1) * 128), 8)
        parts["wpg%d" % m] = _ent_k(w_pg, ar(m * 128, (m + 1) * 128), 8)
    for j in range(22):
        parts["wg%d" % j] = _ent_k(w_gate, ar(j * 128, (j + 1) * 128), 8)
        parts["wu%d" % j] = _ent_k(w_up, ar(j * 128, (j + 1) * 128), 8)
    for hf in range(2):
        for m in range(8):
            parts["wd%d_%d" % (hf, m)] = _ent_k(w_down[hf * 1408:(hf + 1) * 1408], ar(m * 128, (m + 1) * 128), 11)
    parts["wpp"] = _ent_k(w_pp, ar(0, 1024), 2)
    wsrc = np.concatenate([parts[e[0]] for e in ENTS], axis=1).astype(np.float32)
    assert wsrc.shape == (128, TOT)

    g_out = np.concatenate([inp["attn_group_norm"][0], inp["conv_group_norm"][0]])
    gsrc = np.concatenate([
        _gain_k(inp["mix_pre_norm"][0], 8, 128), _gain_k(inp["mix_pre_norm"][0], 8, 64),
        _gain_k(inp["q_norm"][0], 2, 128), _gain_k(inp["kv_norm"][0], 2, 128), _gain_k(inp["kv_norm"][0], 2, 512),
        _gain_k(g_out, 8, 128), _gain_k(inp["ffn_pre_norm"][0], 8, 128)], axis=1).astype(np.float32)
    assert gsrc.shape == (128, TOTG)

    cst = np.zeros((128, NCST), np.float32)
    cst[:, C_IDENT:C_IDENT + 128] = np.eye(128, dtype=np.float32)
    cst[:, C_MASK:C_MASK + 128] = np.triu(np.ones((128, 128), np.float32))
    cst[:, C_GPOST:C_GPOST + 8] = np.asarray(inp["mix_post_norm"][0]).reshape(8, 128).T
    cst[:, C_GFFN:C_GFFN + 8] = np.asarray(inp["ffn_post_norm"][0]).reshape(8, 128).T
    cst[:, C_GPLE:C_GPLE + 8] = np.asarray(inp["ple_norm"][0]).reshape(8, 128).T
    cst[:, C_GFFNPRE:C_GFFNPRE + 8] = np.asarray(inp["ffn_pre_norm"][0]).reshape(8, 128).T
    cw = np.asarray(inp["conv_w"][0])
    for k in range(3):
        cst[:, C_CONVW + k * 4:C_CONVW + k * 4 + 4] = cw[k].reshape(4, 128).T
    half = 16
    inv_freq = (1.0 / (np.float32(10000.0) ** (np.arange(half, dtype=np.float32) / np.float32(half)))).astype(np.float32)
    pidx = np.arange(128)
    cst[:, C_INVF] = inv_freq[pidx % 16]
    cst[:, C_PHASE] = np.where((pidx // 32) % 2 == 0, np.float32(math.pi / 2), np.float32(0.0))
    return wsrc, gsrc, cst


_NC_CACHE = {}


def kernel(**inputs):
    inp = {k: np.asarray(v) for k, v in inputs.items()}
    wsrc, gsrc, cst = host_layout(inp)
    x = inp["x"]
    p = inp["p"][0]
    pos = inp["positions"].astype(np.int32)
    if "nc" not in _NC_CACHE:
        _NC_CACHE["nc"] = build_nc()
    nc = _NC_CACHE["nc"]
    in_maps = []
    for i in range(NCORES):
        sl = slice(i * NSEQ, (i + 1) * NSEQ)
        in_maps.append({"x": np.ascontiguousarray(x[sl]), "p": np.ascontiguousarray(p[sl]),
                        "pos": np.ascontiguousarray(pos[sl]), "wsrc": wsrc, "gsrc": gsrc, "cst": cst})
    res = run_bass_kernel_spmd(nc, in_maps, core_ids=list(range(NCORES)))
    return np.concatenate([r["out"] for r in res.results], axis=0).astype(np.float32)
```

```python
import math
from contextlib import ExitStack

import numpy as np

import concourse.bass as bass
import concourse.mybir as mybir
from concourse.bass_utils import run_bass_kernel_spmd

F32 = mybir.dt.float32
BF16 = mybir.dt.bfloat16
I32 = mybir.dt.int32
AF = mybir.ActivationFunctionType
ALU = mybir.AluOpType

NCORES = 8
NSEQ = 4
S = 2048
D = 1024
PLE = 256
CH = 512
NCH = S // CH
DFF = 2816
NH = 8
EPS = 1e-6
QSCALE = 96.0 ** -0.5
O1, O2, O3, O4, O5 = 256, 512, 544, 1056, 1568

ENGS = ("pe", "act", "dve", "pool", "sp")
import os as _os
KSKIP = _os.environ.get("KSKIP", "").split(",")


class T:
    __slots__ = ("name", "last_w", "readers")

    def __init__(self, name=""):
        self.name = name
        self.last_w = None
        self.readers = []


class Ev:
    __slots__ = ("sem", "val", "eng", "clock")

    def __init__(self, sem, val, eng, clock):
        self.sem = sem
        self.val = val
        self.eng = eng
        self.clock = clock


class Sched:
    def __init__(self, n_dma_sems=8, same_engine_sync=True):
        self.streams = {e: [] for e in ENGS}
        self.cnt = {e: 0 for e in ENGS}
        self.known = {e: {} for e in ENGS}
        self.same_engine_sync = same_engine_sync
        self.n_dma_sems = n_dma_sems
        self.dma_rr = {e: 0 for e in ENGS}
        self.dma_val = {}
        self.dma_last = {}
        self.last_ev = {e: None for e in ENGS}
        self.label = ""
        self.labels = {e: [] for e in ENGS}

    def _wait(self, eng, ev):
        k = self.known[eng]
        if k.get(ev.sem, 0) >= ev.val:
            return
        self.streams[eng].append(("wait", ev.sem, ev.val))
        for s, v in ev.clock.items():
            if k.get(s, 0) < v:
                k[s] = v
        if k.get(ev.sem, 0) < ev.val:
            k[ev.sem] = ev.val

    def _deps(self, eng, reads, writes, extra):
        deps = []
        for t in reads:
            if t.last_w is not None:
                deps.append((t.last_w, "raw"))
        for t in writes:
            if t.last_w is not None:
                deps.append((t.last_w, "waw"))
            for r in t.readers:
                deps.append((r, "war"))
        for ev in extra:
            deps.append((ev, "raw"))
        for ev, kind in deps:
            if ev.eng == eng:
                if eng == "pe":
                    continue
                if not self.same_engine_sync:
                    continue
            self._wait(eng, ev)

    def _finish(self, ev, reads, writes):
        for t in reads:
            t.readers.append(ev)
        for t in writes:
            t.last_w = ev
            t.readers = []

    def op(self, eng, fn, reads=(), writes=(), extra=()):
        self._deps(eng, reads, writes, extra)
        self.cnt[eng] += 1
        sem = "e_" + eng
        val = self.cnt[eng]
        clock = dict(self.known[eng])
        clock[sem] = val
        ev = Ev(sem, val, eng, clock)
        self.streams[eng].append(("op", fn, sem))
        self.labels[eng].append(self.label)
        self._finish(ev, reads, writes)
        self.last_ev[eng] = ev
        return ev

    def dma(self, eng, fn, reads=(), writes=(), extra=(), own_sem=None):
        self._deps(eng, reads, writes, extra)
        if own_sem is not None:
            sem = own_sem
        else:
            i = self.dma_rr[eng]
            self.dma_rr[eng] = (i + 1) % self.n_dma_sems
            sem = "d_%s_%d" % (eng, i)
        prev = self.dma_last.get(sem)
        if prev is not None:
            self._wait(eng, prev)
        val = self.dma_val.get(sem, 0) + 16
        self.dma_val[sem] = val
        clock = dict(self.known[eng])
        ev = Ev(sem, val, "dma_" + eng, clock)
        self.dma_last[sem] = ev
        self.streams[eng].append(("dma", fn, sem))
        self._finish(ev, reads, writes)
        return ev

    def barrier(self, engs=ENGS):
        evs = [self.last_ev[e] for e in ENGS if self.last_ev[e] is not None]
        evs += list(self.dma_last.values())
        for e in engs:
            for ev in evs:
                if ev.eng != e:
                    self._wait(e, ev)

    def sem_keys(self):
        keys = set()
        for e in ENGS:
            for it in self.streams[e]:
                keys.add(it[1] if it[0] == "wait" else it[2])
        return sorted(keys)

    def emit(self, block, sems):
        streams = self.streams

        def replay(name):
            def _f(engobj):
                for it in streams[name]:
                    if it[0] == "wait":
                        engobj.wait_ge(sems[it[1]], it[2])
                    elif it[0] == "op":
                        it[1](engobj).then_inc(sems[it[2]], 1)
                    else:
                        it[1](engobj).then_inc(sems[it[2]], 16)
            return _f

        block.tensor(replay("pe"))
        block.scalar(replay("act"))
        block.vector(replay("dve"))
        block.gpsimd(replay("pool"))
        block.sync(replay("sp"))


class Rot:
    def __init__(self, bufs, name="rot"):
        self.bufs = bufs
        self.ts = [T("%s%d" % (name, i)) for i in range(len(bufs))]
        self.i = 0

    def next(self):
        i = self.i
        self.i = (i + 1) % len(self.bufs)
        return self.bufs[i], self.ts[i]


def MM(out, lhsT, rhs, start, stop):
    return lambda e: e.matmul(out, lhsT, rhs, start=start, stop=stop)


def TR(out, in_, ident):
    return lambda e: e.transpose(out, in_, ident)


def ACTV(out, in_, func, bias=None, scale=None):
    kw = {}
    if bias is not None:
        kw["bias"] = bias
    if scale is not None:
        kw["scale"] = scale
    return lambda e: e.activation(out=out, in_=in_, func=func, **kw)


def TT(out, in0, in1, op):
    return lambda e: e.tensor_tensor(out=out, in0=in0, in1=in1, op=op)


def TS(out, in0, s1, op0, s2=None, op1=None):
    if op1 is None:
        return lambda e: e.tensor_scalar(out=out, in0=in0, scalar1=s1, scalar2=None, op0=op0)
    return lambda e: e.tensor_scalar(out=out, in0=in0, scalar1=s1, scalar2=s2, op0=op0, op1=op1)


def STT(out, in0, scalar, in1, op0, op1):
    return lambda e: e.scalar_tensor_tensor(out=out, in0=in0, scalar=scalar, in1=in1, op0=op0, op1=op1)


def CP(out, in_):
    return lambda e: e.tensor_copy(out=out, in_=in_)


def MSET(ap, v):
    return lambda e: e.memset(ap, v)


def DMA(out, in_):
    return lambda e: e.dma_start(out=out, in_=in_)


QKV_ORDER = [("q", 0), ("k", 0), ("q", 1), ("k", 1), ("q", 2), ("k", 2), ("q", 3), ("k", 3),
             ("q", 4), ("v", 0), ("v", 1), ("q", 5), ("v", 2), ("v", 3), ("q", 6), ("q", 7)]


def stream_entries():
    ents = []
    for m in range(4):
        ents.append(("win%d" % m, 8, 128, "pre", None))
    ents.append(("wkr", 8, 64, "pre64", (32, 48)))
    for fc in range(2):
        ents.append(("wc%d" % fc, 8, 128, "pre", None))
        ents.append(("wx%d" % fc, 8, 128, "pre", None))
        ents.append(("wb%d" % fc, 8, 128, "pre", None))
    for step in QKV_ORDER:
        if step[0] == "q":
            ents.append(("wuq%d" % step[1], 2, 128, "q", (96, 112)))
        elif step[0] == "k":
            ents.append(("wuk%d" % step[1], 2, 128, "kv", None))
        elif step == ("v", 0):
            ents.append(("wuv", 2, 512, "kv512", None))
    for fc in range(2, 4):
        ents.append(("wc%d" % fc, 8, 128, "pre", None))
        ents.append(("wx%d" % fc, 8, 128, "pre", None))
        ents.append(("wb%d" % fc, 8, 128, "pre", None))
    for m in range(8):
        ents.append(("wout%d" % m, 8, 128, "out", None))
    for hf in range(2):
        for jj in range(11):
            ents.append(("wg%d" % (hf * 11 + jj), 8, 128, None, None))
            ents.append(("wu%d" % (hf * 11 + jj), 8, 128, None, None))
        for m in range(8):
            ents.append(("wd%d_%d" % (hf, m), 11, 128, None, None))
    ents.append(("wpp", 2, 1024, None, None))
    for m in range(8):
        ents.append(("wpg%d" % m, 8, 128, None, None))
    return ents


ENTS = stream_entries()
ENT_OFF = {}
_o = 0
for _e in ENTS:
    ENT_OFF[_e[0]] = _o
    _o += _e[1] * _e[2]
TOT = _o
GTYPES = {"pre": (8, 128), "pre64": (8, 64), "q": (2, 128), "kv": (2, 128), "kv512": (2, 512),
          "out": (8, 128), "ffn": (8, 128)}
G_OFF = {}
_o = 0
for _k, (_a, _b) in GTYPES.items():
    G_OFF[_k] = _o
    _o += _a * _b
TOTG = _o
N1 = [e[0] for e in ENTS].index("wg0")

C_IDENT = 0
C_MASK = 128
C_GPOST = 256
C_GFFN = 264
C_GPLE = 272
C_CONVW = 280
C_INVF = 292
C_PHASE = 293
C_GFFNPRE = 294
NCST = 302


def build_nc(nseq=NSEQ, nch=NCH, stop=99):
    nc = bass.Bass("TRN2", target_bir_lowering=False)
    x_d = nc.dram_tensor("x", [NSEQ, S, D], F32, kind="ExternalInput").ap()
    p_d = nc.dram_tensor("p", [NSEQ, S, PLE], F32, kind="ExternalInput").ap()
    pos_d = nc.dram_tensor("pos", [NSEQ, S], I32, kind="ExternalInput").ap()
    wsrc_d = nc.dram_tensor("wsrc", [128, TOT], F32, kind="ExternalInput").ap()
    gsrc_d = nc.dram_tensor("gsrc", [128, TOTG], F32, kind="ExternalInput").ap()
    cst_d = nc.dram_tensor("cst", [128, NCST], F32, kind="ExternalInput").ap()
    out_d = nc.dram_tensor("out", [NSEQ, S, D], F32, kind="ExternalOutput").ap()
    wscr_d = nc.dram_tensor("wscr", [128, TOT], BF16, kind="Internal").ap()

    Sd = Sched()
    es = ExitStack()
    with es:
        def sb(name, shape, dt):
            return es.enter_context(nc.sbuf_tensor(name, shape, dt))

        def ps(name, shape, dt):
            return es.enter_context(nc.psum_tensor(name, shape, dt))

        cst = sb("cst_sb", [128, NCST], F32)
        t_cst = T("cst")
        Sd.dma("sp", DMA(cst[:], cst_d), writes=[t_cst])
        ident = cst[:, C_IDENT:C_IDENT + 128]
        ones_bf = sb("ones_bf", [128, 128], BF16)
        mask_bf = sb("mask_bf", [128, 128], BF16)
        t_ones = T("ones")
        t_mask = T("mask")
        Sd.op("dve", MSET(ones_bf[:], 1.0), writes=[t_ones])
        Sd.op("dve", CP(mask_bf[:], cst[:, C_MASK:C_MASK + 128]), reads=[t_cst], writes=[t_mask])

        t_scr = T("wscr")
        with ExitStack() as pes:
            gt = pes.enter_context(nc.sbuf_tensor("gt", [128, TOTG], F32))
            t_gt = T("gt")
            Sd.dma("sp", DMA(gt[:], gsrc_d), writes=[t_gt])
            NST = 3
            stf = [pes.enter_context(nc.sbuf_tensor("stf%d" % i, [128, 2048], F32)) for i in range(NST)]
            stb = [pes.enter_context(nc.sbuf_tensor("stb%d" % i, [128, 2048], BF16)) for i in range(NST)]
            t_stf = [T("stf%d" % i) for i in range(NST)]
            t_stb = [T("stb%d" % i) for i in range(NST)]
            geng = ["dve", "pool"]
            gi = 0
            for ei, (name, nk, ncols, gtype, neg) in enumerate(ENTS[:N1]):
                sz = nk * ncols
                off = ENT_OFF[name]
                i = ei % NST
                Sd.dma("sp", DMA(stf[i][:, 0:sz], wsrc_d[:, off:off + sz]), writes=[t_stf[i]])
                if gtype is None:
                    Sd.op("act", ACTV(stb[i][:, 0:sz], stf[i][:, 0:sz], AF.Copy),
                          reads=[t_stf[i]], writes=[t_stb[i]])
                else:
                    go = G_OFF[gtype]
                    eng = geng[gi % 2] if sz <= 1024 else "dve"
                    gi += 1
                    Sd.op(eng, TT(stb[i][:, 0:sz], stf[i][:, 0:sz], gt[:, go:go + sz], ALU.mult),
                          reads=[t_stf[i], t_gt], writes=[t_stb[i]])
                if neg is not None:
                    v = stb[i][:, 0:sz].rearrange("p (k c) -> p k c", k=nk)[:, :, neg[0]:neg[1]]
                    Sd.op("act", ACTV(v, v, AF.Copy, scale=-1.0), reads=[t_stb[i]], writes=[t_stb[i]])
                Sd.dma("act", DMA(wscr_d[:, off:off + sz], stb[i][:, 0:sz]), reads=[t_stb[i]], writes=[t_scr])
        Sd.barrier()

        RING = 10240
        ring = sb("ring", [128, RING], BF16)
        xin = [sb("xin%d" % i, [128, D], F32) for i in range(2)]
        xin_rot = Rot(xin, "xin")
        xout_rot = Rot([sb("xout%d" % i, [128, D], F32) for i in range(2)], "xout")
        xTs = [sb("xT%d" % i, [128, 8, CH], F32) for i in range(2)]
        t_xTs = [[T("xT%d_%d" % (j, i)) for i in range(8)] for j in range(2)]
        hT = sb("hT", [128, 8, CH], BF16)
        t_hT = [T("hT%d" % i) for i in range(8)]
        yT = sb("yT", [128, 8, CH], BF16)
        t_yT = [T("yT%d" % i) for i in range(8)]
        sq_rot = Rot([sb("sq%d" % i, [128, CH], BF16) for i in range(3)], "sq")
        rbc_x = sb("rbc_x", [128, CH], F32)
        rbc_q = sb("rbc_q", [128, CH], F32)
        rbc_kv = sb("rbc_kv", [128, CH], F32)
        rbc_o = sb("rbc_o", [128, CH], F32)
        t_rbc_x, t_rbc_q, t_rbc_kv, t_rbc_o = T("rbx"), T("rbq"), T("rbkv"), T("rbo")
        lnt = sb("lnt", [128, CH], F32)
        t_lnt = T("lnt")
        cqT = sb("cqT", [128, 2, CH], BF16)
        ckvT = sb("ckvT", [128, 2, CH], BF16)
        t_cq = [T("cq0"), T("cq1")]
        t_ckv = [T("ckv0"), T("ckv1")]
        uext_rot = Rot([sb("uext%d" % i, [128, CH + 2], F32) for i in range(2)], "uext")
        halo = sb("halo", [128, 4, 2], F32)
        t_halo = [T("halo%d" % i) for i in range(4)]
        tmpf_rot = Rot([sb("tmpf%d" % i, [128, CH], F32) for i in range(5)], "tmpf")
        QT = sb("QT", [128, NH, CH], BF16)
        t_QT = [T("QT%d" % h) for h in range(NH)]
        KT = sb("KT", [128, NH, S], BF16)
        t_KT = [[T("KT%d_%d" % (h, c)) for c in range(NCH)] for h in range(NH)]
        VA = sb("VA", [128, S // 128, 4, 192], BF16)
        t_V = [T("V%d" % c) for c in range(NCH)]
        PT_rot = Rot([sb("PT%d" % i, [128, CH], BF16) for i in range(3)], "PT")
        ROT = sb("ROT_sb", [128, CH], F32)
        t_ROT = T("ROT")
        posi = sb("posi", [128, CH], I32)
        t_posi = T("posi")
        actb = sb("actb", [128, 11, CH], BF16)
        t_act = [T("act%d" % i) for i in range(11)]
        sil_rot = Rot([sb("sil%d" % i, [128, CH], BF16) for i in range(2)], "sil")
        pin_rot = Rot([sb("pin%d" % i, [128, PLE], F32) for i in range(4)], "pin")
        pT = sb("pT", [128, 2, CH], BF16)
        t_pT = T("pT")
        t_out = T("out")

        PS = [ps("ps%d" % i, [128, 512], F32) for i in range(8)]
        ps_rot = Rot(PS[0:5], "psr")
        pso_rot = Rot(PS[5:7], "pso")
        PSN = PS[7]
        t_PSN = T("psn")

        Sd.op("pool", MSET(QT[96:128, :, :], 0.0), writes=t_QT)
        Sd.op("pool", MSET(KT[96:128, :, :], 0.0), writes=[t for row in t_KT for t in row])
        for c in range(NCH):
            Sd.op("pool", MSET(VA[:, c * 4:(c + 1) * 4, :, 64:128], 1.0), writes=[t_V[c]])

        p2 = list(ENTS[N1:])
        t_scr_e = {e[0]: T("scr_" + e[0]) for e in p2}
        p2_stored = set()
        p2_k = [0]

        def pump(n):
            lab = Sd.label
            Sd.label = "prep2"
            for _ in range(n):
                k = p2_k[0]
                if k >= len(p2):
                    break
                name, nk, ncols, gtype, neg = p2[k]
                sz, off = nk * ncols, ENT_OFF[name]
                Sd.dma("pool", DMA(wscr_d[:, off:off + sz], wsrc_d[:, off:off + sz]), writes=[t_scr_e[name]],
                       own_sem="d2_%d" % k)
                p2_stored.add(name)
                p2_k[0] += 1
            Sd.label = lab

        class Ring:
            def __init__(self):
                self.live = []
                self.head = 0
                self.issued = 0
                self.got = 0
                self.slots = {}

            def _try_issue(self):
                idx = self.issued
                name, nk, ncols, _, _ = ENTS[idx % len(ENTS)]
                if name in t_scr_e and name not in p2_stored:
                    return False
                sz = nk * ncols
                start = self.head
                if start + sz > RING:
                    start = 0
                end = start + sz
                ov = [l for l in self.live if not (l[1] <= start or l[0] >= end)]
                for l in ov:
                    if l[3] >= self.got - 1:
                        return False
                extra = []
                for l in ov:
                    extra += l[2].readers
                    if l[2].last_w is not None:
                        extra.append(l[2].last_w)
                    self.live.remove(l)
                t = T("w%d" % idx)
                off = ENT_OFF[name]
                Sd.dma("sp", DMA(ring[:, start:end], wscr_d[:, off:off + sz]), reads=[t_scr_e.get(name, t_scr)],
                       writes=[t], extra=extra)
                self.live.append((start, end, t, idx))
                self.slots[idx] = (start, t, nk, ncols)
                self.head = end
                self.issued += 1
                return True

            def fill(self, total):
                while self.issued < total and self.issued < self.got + 40:
                    if not self._try_issue():
                        break

            def get(self, name, total):
                idx = self.got
                assert ENTS[idx % len(ENTS)][0] == name, (ENTS[idx % len(ENTS)][0], name)
                self.fill(total)
                assert idx in self.slots, "ring too small for " + name
                start, t, nk, ncols = self.slots.pop(idx)
                self.got += 1
                v = ring[:, start:start + nk * ncols].rearrange("p (k c) -> p k c", k=nk)
                return v, t

        R = Ring()
        TOTAL_ENT = nseq * nch * len(ENTS)

        def getw(name):
            return R.get(name, TOTAL_ENT)

        def sumsq_bcast(srcs, dst_rbc, t_dst, n, extra_scale=1.0, keep_sq=None, sq_eng="act"):
            nsrc = len(srcs)
            for i, (ap, ts) in enumerate(srcs):
                if keep_sq is not None:
                    sqb, t_sq = keep_sq[i]
                else:
                    sqb, t_sq = sq_rot.next()
                    sqb = sqb[:]
                if sq_eng == "act":
                    Sd.op("act", ACTV(sqb, ap, AF.Square), reads=ts, writes=[t_sq])
                else:
                    Sd.op("dve", TT(sqb, ap, ap, ALU.mult), reads=ts, writes=[t_sq])
                Sd.op("pe", MM(PSN[:], ones_bf[:], sqb, i == 0, i == nsrc - 1), reads=[t_sq, t_ones],
                      writes=[t_PSN])
            Sd.op("act", ACTV(lnt[:], PSN[:], AF.Ln, bias=EPS, scale=1.0 / n), reads=[t_PSN], writes=[t_lnt])
            Sd.op("act", ACTV(dst_rbc, lnt[:], AF.Exp, scale=-0.5),
                  reads=[t_lnt], writes=[t_dst])

        evac_flip = [0]

        def evac(out, in_, reads, writes):
            evac_flip[0] ^= 1
            if evac_flip[0]:
                Sd.op("act", ACTV(out, in_, AF.Copy), reads=reads, writes=writes)
            else:
                Sd.op("dve", CP(out, in_), reads=reads, writes=writes)

        xpre = {}

        def eng8(i):
            return "dve"

        def rope_table(s, c):
            Sd.label = "rope"
            tok0 = c * CH
            Sd.dma("sp", DMA(posi[:], pos_d[s, tok0:tok0 + CH].partition_broadcast(128)), writes=[t_posi])
            a0, ta0 = tmpf_rot.next()
            a1, ta1 = tmpf_rot.next()
            a2, ta2 = tmpf_rot.next()
            Sd.op("pool", CP(a0[:], posi[:]), reads=[t_posi], writes=[ta0])
            Sd.op("pool", TS(a0[:], a0[:], cst[:, C_INVF:C_INVF + 1], ALU.mult,
                             cst[:, C_PHASE:C_PHASE + 1], ALU.add), reads=[ta0, t_cst], writes=[ta0])
            Sd.op("pool", TS(a1[:], a0[:], 1.0 / (2 * math.pi), ALU.mult, 0.0, ALU.add), reads=[ta0], writes=[ta1])
            Sd.op("pool", CP(posi[:], a1[:]), reads=[ta1], writes=[t_posi])
            Sd.op("pool", CP(a1[:], posi[:]), reads=[t_posi], writes=[ta1])
            Sd.op("dve", STT(a0[:], a1[:], -2 * math.pi, a0[:], ALU.mult, ALU.add), reads=[ta0, ta1], writes=[ta0])
            Sd.op("dve", TS(a2[:], a0[:], math.pi, ALU.is_gt, -2 * math.pi, ALU.mult), reads=[ta0], writes=[ta2])
            Sd.op("dve", TT(a0[:], a0[:], a2[:], ALU.add), reads=[ta0, ta2], writes=[ta0])
            Sd.op("dve", TS(a2[:], a0[:], -math.pi, ALU.is_lt, 2 * math.pi, ALU.mult), reads=[ta0], writes=[ta2])
            Sd.op("dve", TT(a0[:], a0[:], a2[:], ALU.add), reads=[ta0, ta2], writes=[ta0])
            Sd.op("act", ACTV(ROT[:], a0[:], AF.Sin), reads=[ta0], writes=[t_ROT])

        def x_tiles(s, c, par, tts, act_only=False):
            Sd.label = "xload"
            for tt in tts:
                if (s, c, tt) in xpre:
                    xb, t_xb = xpre.pop((s, c, tt))
                else:
                    xb, t_xb = xin_rot.next()
                    Sd.dma("sp", DMA(xb[:], x_d[s, c * CH + tt * 128:c * CH + (tt + 1) * 128, :]), writes=[t_xb])
                for half in range(2):
                    pb, t_pb = ps_rot.next()
                    for q4 in range(4):
                        fc = half * 4 + q4
                        Sd.op("pe", TR(pb[:, q4 * 128:(q4 + 1) * 128], xb[:, fc * 128:(fc + 1) * 128], ident),
                              reads=[t_xb, t_cst], writes=[t_pb])
                    if act_only:
                        Sd.op("act", ACTV(xTs[par][:, half * 4:half * 4 + 4, tt * 128:(tt + 1) * 128],
                                          pb[:, :].rearrange("p (a b) -> p a b", a=4), AF.Copy), reads=[t_pb],
                              writes=t_xTs[par][half * 4:half * 4 + 4])
                    else:
                        evac(xTs[par][:, half * 4:half * 4 + 4, tt * 128:(tt + 1) * 128],
                             pb[:, :].rearrange("p (a b) -> p a b", a=4), [t_pb], t_xTs[par][half * 4:half * 4 + 4])

        def x_prefetch(s, c, tts):
            for tt in tts:
                xb, t_xb = xin_rot.next()
                Sd.dma("sp", DMA(xb[:], x_d[s, c * CH + tt * 128:c * CH + (tt + 1) * 128, :]), writes=[t_xb])
                xpre[(s, c, tt)] = (xb, t_xb)

        def chunk_after(s, c):
            return (s, c + 1) if c + 1 < nch else ((s + 1, 0) if s + 1 < nseq else None)

        def front_A(s, c, par, act_only=False):
            x_tiles(s, c, par, (0, 1), act_only)
            x_prefetch(s, c, (2, 3))

        def front_B(s, c, par, act_only=False):
            x_tiles(s, c, par, (2, 3), act_only)
            n2 = chunk_after(s, c)
            if n2 is not None:
                x_prefetch(n2[0], n2[1], (0, 1))
            Sd.label = "prenorm"
            sumsq_bcast([(xTs[par][:, fc, :], [t_xTs[par][fc]]) for fc in range(8)], rbc_kv[:], t_rbc_kv, D)

        front_A(0, 0, 0)
        front_B(0, 0, 0)
        rope_table(0, 0)
        h_ready = [False]
        for s in range(nseq):
            for c in range(nch):
                tok0 = c * CH
                first_chunk = (s == 0 and c == 0)

                def P(n):
                    if first_chunk:
                        pump(n)

                par = (s * nch + c) % 2
                xT = xTs[par]
                t_xT = t_xTs[par]
                nxt = chunk_after(s, c)
                pins = []
                for tt in range(4):
                    pbuf, t_pbuf = pin_rot.next()
                    Sd.dma("act", DMA(pbuf[:], p_d[s, tok0 + tt * 128:tok0 + (tt + 1) * 128, :]), writes=[t_pbuf])
                    pins.append((pbuf, t_pbuf))
                if stop <= 2:
                    continue
                Sd.label = "prenorm"
                if not h_ready[0]:
                    for fc in range(8):
                        Sd.op("dve", TT(hT[:, fc, :], xT[:, fc, :], rbc_kv[:], ALU.mult),
                              reads=[t_xT[fc], t_rbc_kv], writes=[t_hT[fc]])
                h_ready[0] = False

                if stop <= 3:
                    continue
                Sd.label = "latents"
                def proj8(wname, rhs_t=hT, rhs_ts=t_hT, M=128):
                    w, t_w = getw(wname)
                    pb, t_pb = ps_rot.next()
                    for kc in range(8):
                        Sd.op("pe", MM(pb[0:M, :], w[:, kc, :], rhs_t[:, kc, :], kc == 0, kc == 7),
                              reads=[t_w, rhs_ts[kc]], writes=[t_pb])
                    return pb, t_pb

                def evac_lat(pb, t_pb):
                    rw, t_rw = tmpf_rot.next()
                    Sd.op("act", ACTV(rw[:], pb[:], AF.Copy), reads=[t_pb], writes=[t_rw])
                    sqb, t_sq = sq_rot.next()
                    Sd.op("act", ACTV(sqb[:], pb[:], AF.Square), reads=[t_pb], writes=[t_sq])
                    return rw, t_rw, sqb, t_sq

                def lat_sum(e, first, last):
                    Sd.op("pe", MM(PSN[:], ones_bf[:], e[2][:], first, last), reads=[e[3], t_ones], writes=[t_PSN])

                e0 = evac_lat(*proj8("win0"))
                e1 = evac_lat(*proj8("win1"))
                pb2 = proj8("win2")
                lat_sum(e0, True, False)
                lat_sum(e1, False, True)
                e2 = evac_lat(*pb2)
                Sd.op("act", ACTV(lnt[:], PSN[:], AF.Ln, bias=EPS, scale=1.0 / 256), reads=[t_PSN], writes=[t_lnt])
                Sd.op("act", ACTV(rbc_q[:], lnt[:], AF.Exp, scale=-0.5), reads=[t_lnt], writes=[t_rbc_q])
                pb3 = proj8("win3")
                for i, e in enumerate((e0, e1)):
                    Sd.op("pool", TT(cqT[:, i, :], e[0][:], rbc_q[:], ALU.mult),
                          reads=[e[1], t_rbc_q], writes=[t_cq[i]])
                e3 = evac_lat(*pb3)

                def finish_kv():
                    lat_sum(e2, True, False)
                    lat_sum(e3, False, True)
                    Sd.op("act", ACTV(lnt[:], PSN[:], AF.Ln, bias=EPS, scale=1.0 / 256), reads=[t_PSN], writes=[t_lnt])
                    Sd.op("act", ACTV(rbc_kv[:], lnt[:], AF.Exp, scale=-0.5), reads=[t_lnt], writes=[t_rbc_kv])
                    for i, e in enumerate((e2, e3)):
                        Sd.op("pool", TT(ckvT[:, i, :], e[0][:], rbc_kv[:], ALU.mult),
                              reads=[e[1], t_rbc_kv], writes=[t_ckv[i]])

                P(6)
                if stop <= 4:
                    continue
                Sd.label = "krope"
                pb, t_pb = proj8("wkr", M=64)
                Sd.label = "latents"
                finish_kv()
                Sd.label = "krope"
                t1, tt1 = tmpf_rot.next()
                t2, tt2 = tmpf_rot.next()
                Sd.op("dve", TT(t1[64:96, :], pb[0:32, :], ROT[0:32, :], ALU.mult), reads=[t_pb, t_ROT], writes=[tt1])
                Sd.op("dve", TT(t2[64:96, :], pb[32:64, :], ROT[32:64, :], ALU.mult), reads=[t_pb, t_ROT], writes=[tt2])
                for h in range(NH):
                    Sd.op("pool", TT(KT[64:96, h, tok0:tok0 + CH], t1[64:96, :], t2[64:96, :], ALU.add),
                          reads=[tt1, tt2], writes=[t_KT[h][c]])

                if stop <= 6:
                    continue
                def conv_fc(fc):
                    Sd.label = "conv"
                    pc, t_pc = proj8("wc%d" % fc)
                    px, t_px = proj8("wx%d" % fc)
                    pbb, t_pbb = proj8("wb%d" % fc)
                    csb, t_csb = tmpf_rot.next()
                    acc, t_acc = tmpf_rot.next()
                    Sd.op("act", ACTV(csb[:], pc[:], AF.Copy), reads=[t_pc], writes=[t_csb])
                    ue, t_ue = uext_rot.next()
                    if c == 0:
                        Sd.op("pool", MSET(ue[:, 0:2], 0.0), writes=[t_ue])
                    else:
                        Sd.op("pool", CP(ue[:, 0:2], halo[:, fc, :]), reads=[t_halo[fc]], writes=[t_ue])
                    Sd.op("dve", TT(ue[:, 2:CH + 2], px[:], csb[:], ALU.mult), reads=[t_px, t_csb],
                          writes=[t_ue])
                    Sd.op("pool", CP(halo[:, fc, :], ue[:, CH:CH + 2]), reads=[t_ue], writes=[t_halo[fc]])
                    cw = lambda k: cst[:, C_CONVW + k * 4 + fc:C_CONVW + k * 4 + fc + 1]
                    Sd.op("pool", TS(acc[:], ue[:, 2:CH + 2], cw(0), ALU.mult, 0.0, ALU.add), reads=[t_ue, t_cst],
                          writes=[t_acc])
                    Sd.op("dve", STT(acc[:], ue[:, 1:CH + 1], cw(1), acc[:], ALU.mult, ALU.add),
                          reads=[t_ue, t_acc, t_cst], writes=[t_acc])
                    Sd.op("dve", STT(acc[:], ue[:, 0:CH], cw(2), acc[:], ALU.mult, ALU.add),
                          reads=[t_ue, t_acc, t_cst], writes=[t_acc])
                    Sd.op("dve", TT(yT[:, 4 + fc, :], pbb[:], acc[:], ALU.mult), reads=[t_pbb, t_acc],
                          writes=[t_yT[4 + fc]])
                    Sd.label = "attn"

                conv_fc(0)
                conv_fc(1)
                P(6)
                def q_head(h):
                    Sd.label = "qproj"
                    w, t_w = getw("wuq%d" % h)
                    pb, t_pb = ps_rot.next()
                    for kc in range(2):
                        Sd.op("pe", MM(pb[:], w[:, kc, :], cqT[:, kc, :], kc == 0, kc == 1),
                              reads=[t_w, t_cq[kc]], writes=[t_pb])
                    t1, tt1 = tmpf_rot.next()
                    t2, tt2 = tmpf_rot.next()
                    Sd.op("dve", TT(t1[64:96, :], pb[64:96, :], ROT[64:96, :], ALU.mult), reads=[t_pb, t_ROT],
                          writes=[tt1])
                    ev2 = Sd.op("dve", TT(t2[64:96, :], pb[96:128, :], ROT[96:128, :], ALU.mult), reads=[t_pb, t_ROT],
                                writes=[tt2])
                    Sd.op("act", ACTV(QT[0:64, h, :], pb[0:64, :], AF.Copy), reads=[t_pb], writes=[t_QT[h]], extra=[ev2])
                    Sd.op("pool", TT(QT[64:96, h, :], t1[64:96, :], t2[64:96, :], ALU.add), reads=[tt1, tt2],
                          writes=[t_QT[h]])

                def k_pair(hp):
                    Sd.label = "kproj"
                    w, t_w = getw("wuk%d" % hp)
                    pb, t_pb = ps_rot.next()
                    for kc in range(2):
                        Sd.op("pe", MM(pb[:], w[:, kc, :], ckvT[:, kc, :], kc == 0, kc == 1),
                              reads=[t_w, t_ckv[kc]], writes=[t_pb])
                    Sd.op("act", ACTV(KT[0:64, 2 * hp, tok0:tok0 + CH], pb[0:64, :], AF.Copy),
                          reads=[t_pb], writes=[t_KT[2 * hp][c]])
                    Sd.op("act", ACTV(KT[0:64, 2 * hp + 1, tok0:tok0 + CH], pb[64:128, :], AF.Copy),
                          reads=[t_pb], writes=[t_KT[2 * hp + 1][c]])

                wv_ = [None]

                def v_tile(tt):
                    Sd.label = "vproj"
                    if wv_[0] is None:
                        wv_[0] = getw("wuv")
                    w, t_w = wv_[0]
                    pb, t_pb = ps_rot.next()
                    for kc in range(2):
                        Sd.op("pe", MM(pb[:], ckvT[:, kc, tt * 128:(tt + 1) * 128], w[:, kc, :], kc == 0, kc == 1),
                              reads=[t_w, t_ckv[kc]], writes=[t_pb])
                    pv4 = pb[:, :].rearrange("p (h b d) -> p h b d", h=4, b=2)
                    for b_ in range(2):
                        Sd.op("act", ACTV(VA[:, c * 4 + tt, :, b_ * 128:b_ * 128 + 64], pv4[:, :, b_, :], AF.Copy),
                              reads=[t_pb], writes=[t_V[c]])

                for step in QKV_ORDER:
                    if step[0] == "q":
                        q_head(step[1])
                    elif step[0] == "k":
                        k_pair(step[1])
                    else:
                        v_tile(step[1])
                P(12)
                P(3)
                if stop <= 9:
                    continue
                Sd.label = "attn"
                nkt = 4 * c + 4
                LA = 2
                seq = [(h, kt) for h in range(NH) for kt in range(nkt)]
                pos = {}

                def score(h, kt):
                    j = kt - 4 * c
                    q0 = max(j, 0) * 128
                    pb, t_pb = ps_rot.next()
                    Sd.op("pe", MM(pb[:, q0:CH], KT[:, h, kt * 128:(kt + 1) * 128], QT[:, h, q0:CH],
                                   True, True), reads=[t_KT[h][kt // 4], t_QT[h]], writes=[t_pb])
                    return pb, t_pb, q0, j

                def expo(sc):
                    pb, t_pb, q0, j = sc
                    ptb, t_pt = PT_rot.next()
                    Sd.op("act", ACTV(ptb[:, q0:CH], pb[:, q0:CH], AF.Exp, scale=QSCALE), reads=[t_pb], writes=[t_pt])
                    if j >= 0:
                        Sd.op("pool", TT(ptb[:, q0:q0 + 128], ptb[:, q0:q0 + 128], mask_bf[:], ALU.mult),
                              reads=[t_pt, t_mask], writes=[t_pt])
                    return ptb, t_pt, q0

                def pv(h, kt, pt):
                    ptb, t_pt, q0 = pt
                    po, t_po = pos[h]
                    Sd.op("pe", MM(po[:, q0:CH], VA[:, kt, h // 2, (h % 2) * 64:(h % 2) * 64 + 128], ptb[:, q0:CH],
                                   kt == 0, kt == nkt - 1),
                          reads=[t_pt, t_V[kt // 4]], writes=[t_po])

                def epilogue(h):
                    po, t_po = pos.pop(h)
                    l1, tl1 = tmpf_rot.next()
                    hp_, ho = h // 2, (h % 2) * 64
                    Sd.op("act", ACTV(l1[ho:ho + 64, :], po[64 - ho:128 - ho, :], AF.Ln), reads=[t_po], writes=[tl1])
                    Sd.op("act", ACTV(l1[ho:ho + 64, :], l1[ho:ho + 64, :], AF.Exp, scale=-1.0), reads=[tl1], writes=[tl1])
                    Sd.op("dve", TT(yT[ho:ho + 64, hp_, :], po[ho:ho + 64, :], l1[ho:ho + 64, :], ALU.mult),
                          reads=[t_po, tl1], writes=[t_yT[hp_]])
                    P(6)
                    if h in (0, 2):
                        conv_fc(h // 2 + 2)
                        if h == 2:
                            Sd.label = "gnorm"
                            sumsq_bcast([(yT[:, 4 + i, :], [t_yT[4 + i]]) for i in range(4)], rbc_o[:], t_rbc_o, 512,
                                        sq_eng="dve")
                            for i in range(4):
                                Sd.op("dve" if i % 2 == 0 else "pool", TT(hT[:, 4 + i, :], yT[:, 4 + i, :], rbc_o[:], ALU.mult),
                                      reads=[t_yT[4 + i], t_rbc_o], writes=[t_hT[4 + i]])
                            Sd.label = "attn"

                nseqq = len(seq)
                conv_iters = [(hh + 1) * nkt for hh in (0, 2)]

                def target(i):
                    t = i + 3
                    for ci in conv_iters:
                        if ci >= i + 1 or ci == i:
                            pass
                    nxt_ci = [ci for ci in conv_iters if ci > i]
                    if nxt_ci:
                        t = min(t, nxt_ci[0] + 1)
                    if i in conv_iters:
                        t = min(t, i + 2)
                    return min(t, nseqq - 1)

                scs = {}
                issued = 0
                t0_ = min(2, nseqq - 1, conv_iters[0] + 1)
                while issued <= t0_:
                    scs[issued] = score(*seq[issued])
                    issued += 1
                pend_ep = None
                for i in range(nseqq):
                    h, kt = seq[i]
                    if kt == 0:
                        pos[h] = pso_rot.next()
                    pt = expo(scs.pop(i))
                    if pend_ep is not None:
                        epilogue(pend_ep)
                        pend_ep = None
                    tg = target(i)
                    while issued <= tg:
                        scs[issued] = score(*seq[issued])
                        issued += 1
                    pv(h, kt, pt)
                    if kt == nkt - 1:
                        pend_ep = h
                epilogue(pend_ep)

                if first_chunk:
                    pump(len(p2) + 2)
                if stop <= 10:
                    continue
                Sd.label = "gnorm"
                sumsq_bcast([(yT[:, i, :], [t_yT[i]]) for i in range(4)], rbc_o[:], t_rbc_o, 512)
                for i in range(4):
                    Sd.op("dve" if i != 3 else "pool", TT(hT[:, i, :], yT[:, i, :], rbc_o[:], ALU.mult),
                          reads=[t_yT[i], t_rbc_o], writes=[t_hT[i]])

                if stop <= 11:
                    continue
                Sd.label = "wout"
                def sq_and_sum(src_ap, src_ts, acc_ps, t_acc, first, last):
                    sqb, t_sq = sq_rot.next()
                    Sd.op("act", ACTV(sqb[:], src_ap, AF.Square), reads=src_ts, writes=[t_sq])
                    return lambda: Sd.op("pe", MM(acc_ps[:], ones_bf[:], sqb[:], first, last), reads=[t_sq, t_ones],
                                         writes=[t_acc])

                def rstd_from(acc_ps, t_acc, dst, t_dst, n):
                    Sd.op("act", ACTV(lnt[:], acc_ps[:], AF.Ln, bias=EPS, scale=1.0 / n), reads=[t_acc], writes=[t_lnt])
                    Sd.op("act", ACTV(dst[:], lnt[:], AF.Exp, scale=-0.5), reads=[t_lnt], writes=[t_dst])

                def residual_add(next_norm):
                    pend = None
                    for m in range(8):
                        e_ = eng8(m)
                        tb, t_tb = tmpf_rot.next()
                        Sd.op(e_, TT(tb[:], yT[:, m, :], rbc_o[:], ALU.mult), reads=[t_yT[m], t_rbc_o], writes=[t_tb])
                        Sd.op(e_, TT(xT[:, m, :], xT[:, m, :], tb[:], ALU.add), reads=[t_tb, t_xT[m]], writes=[t_xT[m]])
                        if next_norm:
                            nxt_ = sq_and_sum(xT[:, m, :], [t_xT[m]], PSN, t_PSN, m == 0, m == 7)
                            if pend is not None:
                                pend()
                            pend = nxt_
                    if pend is not None:
                        pend()

                def gcol(base, m):
                    return cst[:, base + m:base + m + 1]

                pend = None
                for m in range(8):
                    pb, t_pb = proj8("wout%d" % m)
                    if pend is not None:
                        pend()
                    pend = sq_and_sum(pb[:], [t_pb], PSN, t_PSN, m == 0, m == 7)
                    Sd.op("act", ACTV(yT[:, m, :], pb[:], AF.Identity, scale=gcol(C_GPOST, m)), reads=[t_pb, t_cst],
                          writes=[t_yT[m]])
                pend()
                rstd_from(PSN, t_PSN, rbc_o, t_rbc_o, D)
                if nxt is not None:
                    front_A(nxt[0], nxt[1], 1 - par, act_only=True)
                    Sd.label = "wout"
                residual_add(True)

                if stop <= 12:
                    continue
                Sd.label = "ffn"
                def xg_op(fc):
                    xg, t_xg = tmpf_rot.next()
                    Sd.op("act", ACTV(xg[:], xT[:, fc, :], AF.Identity, scale=gcol(C_GFFNPRE, fc)),
                          reads=[t_xT[fc], t_cst], writes=[t_xg])
                    return xg, t_xg
                xgs = [xg_op(fc) for fc in range(3)]
                rstd_from(PSN, t_PSN, rbc_x, t_rbc_x, D)
                for fc in range(8):
                    xg, t_xg = xgs[fc]
                    Sd.op("dve", TT(hT[:, fc, :], xg[:], rbc_x[:], ALU.mult),
                          reads=[t_xg, t_rbc_x], writes=[t_hT[fc]])
                    if fc + 3 < 8:
                        xgs.append(xg_op(fc + 3))
                ffn_pend = []
                for hf in range(2):
                    for jj in range(11):
                        j = hf * 11 + jj
                        pg, t_pg = proj8("wg%d" % j)
                        pu, t_pu = proj8("wu%d" % j)
                        sl, t_sl = sil_rot.next()
                        Sd.op("act", ACTV(sl[:], pg[:], AF.Silu), reads=[t_pg], writes=[t_sl])
                        Sd.op("dve", TT(actb[:, jj, :], pu[:], sl[:], ALU.mult), reads=[t_pu, t_sl], writes=[t_act[jj]])
                    for m in range(8):
                        w, t_w = getw("wd%d_%d" % (hf, m))
                        pb, t_pb = ps_rot.next()
                        for jj in range(11):
                            Sd.op("pe", MM(pb[:], w[:, jj, :], actb[:, jj, :], jj == 0, jj == 10),
                                  reads=[t_w, t_act[jj]], writes=[t_pb])
                        if hf == 0:
                            Sd.op("act", ACTV(yT[:, m, :], pb[:], AF.Copy), reads=[t_pb], writes=[t_yT[m]])
                        else:
                            Sd.op("dve", TT(yT[:, m, :], pb[:], yT[:, m, :], ALU.add), reads=[t_pb, t_yT[m]],
                                  writes=[t_yT[m]])
                            nxt_ = sq_and_sum(yT[:, m, :], [t_yT[m]], PSN, t_PSN, m == 0, m == 7)
                            for f_ in ffn_pend:
                                f_()
                            ffn_pend = [nxt_]
                            Sd.op("act", ACTV(yT[:, m, :], yT[:, m, :], AF.Identity, scale=gcol(C_GFFN, m)),
                                  reads=[t_yT[m], t_cst], writes=[t_yT[m]])
                    if hf == 0 and nxt is not None:
                        rope_table(nxt[0], nxt[1])
                        Sd.label = "ffn"

                if stop <= 13:
                    continue
                Sd.label = "ple"
                for tt in range(4):
                    pbuf, t_pbuf = pins[tt]
                    pb, t_pb = ps_rot.next()
                    for i in range(2):
                        Sd.op("pe", TR(pb[:, i * 128:(i + 1) * 128], pbuf[:, i * 128:(i + 1) * 128], ident),
                              reads=[t_pbuf, t_cst], writes=[t_pb])
                    Sd.op("dve", CP(pT[:, :, tt * 128:(tt + 1) * 128], pb[:, 0:256].rearrange("p (a b) -> p a b", a=2)),
                          reads=[t_pb], writes=[t_pT])
                Sd.label = "ffn"
                for f_ in ffn_pend:
                    f_()
                rstd_from(PSN, t_PSN, rbc_o, t_rbc_o, D)
                Sd.label = "ple"
                w, t_w = getw("wpp")
                PSE, t_PSE = pso_rot.next()
                pend = None
                for m in range(8):
                    pb, t_pb = ps_rot.next()
                    for kc in range(2):
                        Sd.op("pe", MM(pb[:], w[:, kc, m * 128:(m + 1) * 128], pT[:, kc, :], kc == 0, kc == 1),
                              reads=[t_w, t_pT], writes=[t_pb])
                    if pend is not None:
                        pend()
                    pend = sq_and_sum(pb[:], [t_pb], PSE, t_PSE, m == 0, m == 7)
                    Sd.op("act", ACTV(actb[:, m, :], pb[:], AF.Identity, scale=gcol(C_GPLE, m)), reads=[t_pb, t_cst],
                          writes=[t_act[m]])
                pend()
                Sd.label = "ffn"
                if nxt is not None:
                    front_B(nxt[0], nxt[1], 1 - par, act_only=True)
                    Sd.label = "ffn"
                residual_add(False)
                Sd.label = "ple"
                rstd_from(PSE, t_PSE, rbc_q, t_rbc_q, D)
                for m in range(8):
                    Sd.op(eng8(m), TT(actb[:, m, :], actb[:, m, :], rbc_q[:], ALU.mult), reads=[t_act[m], t_rbc_q],
                          writes=[t_act[m]])
                for fc in range(8):
                    if fc % 2 == 0:
                        Sd.op("act", ACTV(hT[:, fc, :], xT[:, fc, :], AF.Copy), reads=[t_xT[fc]], writes=[t_hT[fc]])
                    else:
                        Sd.op(eng8(fc), CP(hT[:, fc, :], xT[:, fc, :]), reads=[t_xT[fc]], writes=[t_hT[fc]])
                for m in range(8):
                    pb, t_pb = proj8("wpg%d" % m)
                    sg, t_sg = tmpf_rot.next()
                    Sd.op("act", ACTV(sg[:], pb[:], AF.Sigmoid), reads=[t_pb], writes=[t_sg])
                    Sd.op("dve", TT(sg[:], sg[:], actb[:, m, :], ALU.mult), reads=[t_sg, t_act[m]], writes=[t_sg])
                    Sd.op(eng8(m), TT(xT[:, m, :], xT[:, m, :], sg[:], ALU.add), reads=[t_sg, t_xT[m]], writes=[t_xT[m]])

                if stop <= 14:
                    continue
                if nxt is not None:
                    Sd.label = "prenorm"
                    for fc in range(8):
                        Sd.op("dve", TT(hT[:, fc, :], xTs[1 - par][:, fc, :], rbc_kv[:], ALU.mult),
                              reads=[t_xTs[1 - par][fc], t_rbc_kv], writes=[t_hT[fc]])
                    h_ready[0] = True
                Sd.label = "store"
                for tt in range(4):
                    xo, t_xo = xout_rot.next()
                    for half in range(2):
                        pb, t_pb = ps_rot.next()
                        for q4 in range(4):
                            fc = half * 4 + q4
                            Sd.op("pe", TR(pb[:, q4 * 128:(q4 + 1) * 128], xT[:, fc, tt * 128:(tt + 1) * 128], ident),
                                  reads=[t_xT[fc], t_cst], writes=[t_pb])
                        if tt < 2:
                            Sd.op("act", ACTV(xo[:, half * 512:(half + 1) * 512], pb[:], AF.Copy), reads=[t_pb],
                                  writes=[t_xo])
                        else:
                            Sd.op("dve", CP(xo[:, half * 512:(half + 1) * 512], pb[:]), reads=[t_pb], writes=[t_xo])
                    Sd.dma("pool", DMA(out_d[s, tok0 + tt * 128:tok0 + (tt + 1) * 128, :], xo[:]), reads=[t_xo],
                           writes=[t_out])

        for ev in Sd.dma_last.values():
            if ev.eng == "dma_pool":
                Sd._wait("pool", ev)

        nc._sched_labels = Sd.labels
        keys = Sd.sem_keys()
        sems = {k: es.enter_context(nc.semaphore(k)) for k in keys}
        block = es.enter_context(nc.Block())
        Sd.emit(block, sems)
    return nc


def _ent_k(W, cols, nk):
    W = np.asarray(W)
    return np.ascontiguousarray(W.reshape(nk, 128, W.shape[1])[:, :, cols].transpose(1, 0, 2)).reshape(128, -1)


def _gain_k(g, nk, ncols):
    g = np.asarray(g, dtype=np.float32).reshape(nk, 128)
    return np.ascontiguousarray(np.broadcast_to(g.T[:, :, None], (128, nk, ncols))).reshape(128, -1)


def host_layout(inp):
    w_in = inp["w_in"][0]
    w_uq = inp["w_uq"][0]
    w_ukv = inp["w_ukv"][0]
    w_out = inp["w_out"][0]
    w_gate = inp["w_gate"][0]
    w_up = inp["w_up"][0]
    w_down = inp["w_down"][0]
    w_pp = inp["w_ple_proj"][0]
    w_pg = inp["w_ple_gate"][0]
    ar = np.arange
    parts = {}
    for m in range(2):
        parts["win%d" % m] = _ent_k(w_in, ar(m * 128, (m + 1) * 128), 8)
        parts["win%d" % (2 + m)] = _ent_k(w_in, O1 + ar(m * 128, (m + 1) * 128), 8)
    krc = np.concatenate([O2 + ar(0, 32), O2 + ar(16, 32), O2 + ar(0, 16)])
    parts["wkr"] = _ent_k(w_in, krc, 8)
    for fc in range(4):
        parts["wb%d" % fc] = _ent_k(w_in, O3 + ar(fc * 128, (fc + 1) * 128), 8)
        parts["wc%d" % fc] = _ent_k(w_in, O4 + ar(fc * 128, (fc + 1) * 128), 8)
        parts["wx%d" % fc] = _ent_k(w_in, O5 + ar(fc * 128, (fc + 1) * 128), 8)
    for h in range(NH):
        b = h * 96
        qc = np.concatenate([b + ar(0, 96), b + ar(80, 96), b + ar(64, 80)])
        parts["wuq%d" % h] = _ent_k(w_uq, qc, 2)
    for hp in range(4):
        kc = np.concatenate([(2 * hp) * 128 + ar(0, 64), (2 * hp + 1) * 128 + ar(0, 64)])
        parts["wuk%d" % hp] = _ent_k(w_ukv, kc, 2)
    vc = np.concatenate([h * 128 + 64 + ar(0, 64) for h in range(NH)])
    parts["wuv"] = _ent_k(w_ukv, vc, 2)
    for m in range(8):
        parts["wout%d" % m] = _ent_k(w_out, ar(m * 128, (m + 1) * 128), 8)
        parts["wpg%d" % m] = _ent_k(w_pg, ar(m * 128, (m + 1) * 128), 8)
    for j in range(22):
        parts["wg%d" % j] = _ent_k(w_gate, ar(j * 128, (j + 1) * 128), 8)
        parts["wu%d" % j] = _ent_k(w_up, ar(j * 128, (j + 1) * 128), 8)
    for hf in range(2):
        for m in range(8):
            parts["wd%d_%d" % (hf, m)] = _ent_k(w_down[hf * 1408:(hf + 1) * 1408], ar(m * 128, (m + 1) * 128), 11)
    parts["wpp"] = _ent_k(w_pp, ar(0, 1024), 2)
    wsrc = np.concatenate([parts[e[0]] for e in ENTS], axis=1).astype(np.float32)
    assert wsrc.shape == (128, TOT)

    g_out = np.concatenate([inp["attn_group_norm"][0], inp["conv_group_norm"][0]])
    gsrc = np.concatenate([
        _gain_k(inp["mix_pre_norm"][0], 8, 128), _gain_k(inp["mix_pre_norm"][0], 8, 64),
        _gain_k(inp["q_norm"][0], 2, 128), _gain_k(inp["kv_norm"][0], 2, 128), _gain_k(inp["kv_norm"][0], 2, 512),
        _gain_k(g_out, 8, 128), _gain_k(inp["ffn_pre_norm"][0], 8, 128)], axis=1).astype(np.float32)
    assert gsrc.shape == (128, TOTG)

    cst = np.zeros((128, NCST), np.float32)
    cst[:, C_IDENT:C_IDENT + 128] = np.eye(128, dtype=np.float32)
    cst[:, C_MASK:C_MASK + 128] = np.triu(np.ones((128, 128), np.float32))
    cst[:, C_GPOST:C_GPOST + 8] = np.asarray(inp["mix_post_norm"][0]).reshape(8, 128).T
    cst[:, C_GFFN:C_GFFN + 8] = np.asarray(inp["ffn_post_norm"][0]).reshape(8, 128).T
    cst[:, C_GPLE:C_GPLE + 8] = np.asarray(inp["ple_norm"][0]).reshape(8, 128).T
    cst[:, C_GFFNPRE:C_GFFNPRE + 8] = np.asarray(inp["ffn_pre_norm"][0]).reshape(8, 128).T
    cw = np.asarray(inp["conv_w"][0])
    for k in range(3):
        cst[:, C_CONVW + k * 4:C_CONVW + k * 4 + 4] = cw[k].reshape(4, 128).T
    half = 16
    inv_freq = (1.0 / (np.float32(10000.0) ** (np.arange(half, dtype=np.float32) / np.float32(half)))).astype(np.float32)
    pidx = np.arange(128)
    cst[:, C_INVF] = inv_freq[pidx % 16]
    cst[:, C_PHASE] = np.where((pidx // 32) % 2 == 0, np.float32(math.pi / 2), np.float32(0.0))
    return wsrc, gsrc, cst


_NC_CACHE = {}


def kernel(**inputs):
    inp = {k: np.asarray(v) for k, v in inputs.items()}
    wsrc, gsrc, cst = host_layout(inp)
    x = inp["x"]
    p = inp["p"][0]
    pos = inp["positions"].astype(np.int32)
    if "nc" not in _NC_CACHE:
        _NC_CACHE["nc"] = build_nc()
    nc = _NC_CACHE["nc"]
    in_maps = []
    for i in range(NCORES):
        sl = slice(i * NSEQ, (i + 1) * NSEQ)
        in_maps.append({"x": np.ascontiguousarray(x[sl]), "p": np.ascontiguousarray(p[sl]),
                        "pos": np.ascontiguousarray(pos[sl]), "wsrc": wsrc, "gsrc": gsrc, "cst": cst})
    res = run_bass_kernel_spmd(nc, in_maps, core_ids=list(range(NCORES)))
    return np.concatenate([r["out"] for r in res.results], axis=0).astype(np.float32)
```

```python
import math
from contextlib import ExitStack

import numpy as np

import concourse.bass as bass
import concourse.mybir as mybir
from concourse.bass_utils import run_bass_kernel_spmd

F32 = mybir.dt.float32
BF16 = mybir.dt.bfloat16
I32 = mybir.dt.int32
AF = mybir.ActivationFunctionType
ALU = mybir.AluOpType

NCORES = 8
NSEQ = 4
S = 2048
D = 1024
PLE = 256
CH = 512
NCH = S // CH
DFF = 2816
NH = 8
EPS = 1e-6
QSCALE = 96.0 ** -0.5
O1, O2, O3, O4, O5 = 256, 512, 544, 1056, 1568

ENGS = ("pe", "act", "dve", "pool", "sp")
import os as _os
KSKIP = _os.environ.get("KSKIP", "").split(",")


class T:
    __slots__ = ("name", "last_w", "readers")

    def __init__(self, name=""):
        self.name = name
        self.last_w = None
        self.readers = []


class Ev:
    __slots__ = ("sem", "val", "eng", "clock")

    def __init__(self, sem, val, eng, clock):
        self.sem = sem
        self.val = val
        self.eng = eng
        self.clock = clock


class Sched:
    def __init__(self, n_dma_sems=8, same_engine_sync=True):
        self.streams = {e: [] for e in ENGS}
        self.cnt = {e: 0 for e in ENGS}
        self.known = {e: {} for e in ENGS}
        self.same_engine_sync = same_engine_sync
        self.n_dma_sems = n_dma_sems
        self.dma_rr = {e: 0 for e in ENGS}
        self.dma_val = {}
        self.dma_last = {}
        self.last_ev = {e: None for e in ENGS}
        self.label = ""
        self.labels = {e: [] for e in ENGS}

    def _wait(self, eng, ev):
        k = self.known[eng]
        if k.get(ev.sem, 0) >= ev.val:
            return
        self.streams[eng].append(("wait", ev.sem, ev.val))
        for s, v in ev.clock.items():
            if k.get(s, 0) < v:
                k[s] = v
        if k.get(ev.sem, 0) < ev.val:
            k[ev.sem] = ev.val

    def _deps(self, eng, reads, writes, extra):
        deps = []
        for t in reads:
            if t.last_w is not None:
                deps.append((t.last_w, "raw"))
        for t in writes:
            if t.last_w is not None:
                deps.append((t.last_w, "waw"))
            for r in t.readers:
                deps.append((r, "war"))
        for ev in extra:
            deps.append((ev, "raw"))
        for ev, kind in deps:
            if ev.eng == eng:
                if eng == "pe":
                    continue
                if not self.same_engine_sync:
                    continue
            self._wait(eng, ev)

    def _finish(self, ev, reads, writes):
        for t in reads:
            t.readers.append(ev)
        for t in writes:
            t.last_w = ev
            t.readers = []

    def op(self, eng, fn, reads=(), writes=(), extra=()):
        self._deps(eng, reads, writes, extra)
        self.cnt[eng] += 1
        sem = "e_" + eng
        val = self.cnt[eng]
        clock = dict(self.known[eng])
        clock[sem] = val
        ev = Ev(sem, val, eng, clock)
        self.streams[eng].append(("op", fn, sem))
        self.labels[eng].append(self.label)
        self._finish(ev, reads, writes)
        self.last_ev[eng] = ev
        return ev

    def dma(self, eng, fn, reads=(), writes=(), extra=(), own_sem=None):
        self._deps(eng, reads, writes, extra)
        if own_sem is not None:
            sem = own_sem
        else:
            i = self.dma_rr[eng]
            self.dma_rr[eng] = (i + 1) % self.n_dma_sems
            sem = "d_%s_%d" % (eng, i)
        prev = self.dma_last.get(sem)
        if prev is not None:
            self._wait(eng, prev)
        val = self.dma_val.get(sem, 0) + 16
        self.dma_val[sem] = val
        clock = dict(self.known[eng])
        ev = Ev(sem, val, "dma_" + eng, clock)
        self.dma_last[sem] = ev
        self.streams[eng].append(("dma", fn, sem))
        self._finish(ev, reads, writes)
        return ev

    def barrier(self, engs=ENGS):
        evs = [self.last_ev[e] for e in ENGS if self.last_ev[e] is not None]
        evs += list(self.dma_last.values())
        for e in engs:
            for ev in evs:
                if ev.eng != e:
                    self._wait(e, ev)

    def sem_keys(self):
        keys = set()
        for e in ENGS:
            for it in self.streams[e]:
                keys.add(it[1] if it[0] == "wait" else it[2])
        return sorted(keys)

    def emit(self, block, sems):
        streams = self.streams

        def replay(name):
            def _f(engobj):
                for it in streams[name]:
                    if it[0] == "wait":
                        engobj.wait_ge(sems[it[1]], it[2])
                    elif it[0] == "op":
                        it[1](engobj).then_inc(sems[it[2]], 1)
                    else:
                        it[1](engobj).then_inc(sems[it[2]], 16)
            return _f

        block.tensor(replay("pe"))
        block.scalar(replay("act"))
        block.vector(replay("dve"))
        block.gpsimd(replay("pool"))
        block.sync(replay("sp"))


class Rot:
    def __init__(self, bufs, name="rot"):
        self.bufs = bufs
        self.ts = [T("%s%d" % (name, i)) for i in range(len(bufs))]
        self.i = 0

    def next(self):
        i = self.i
        self.i = (i + 1) % len(self.bufs)
        return self.bufs[i], self.ts[i]


def MM(out, lhsT, rhs, start, stop):
    return lambda e: e.matmul(out, lhsT, rhs, start=start, stop=stop)


def TR(out, in_, ident):
    return lambda e: e.transpose(out, in_, ident)


def ACTV(out, in_, func, bias=None, scale=None):
    kw = {}
    if bias is not None:
        kw["bias"] = bias
    if scale is not None:
        kw["scale"] = scale
    return lambda e: e.activation(out=out, in_=in_, func=func, **kw)


def TT(out, in0, in1, op):
    return lambda e: e.tensor_tensor(out=out, in0=in0, in1=in1, op=op)


def TS(out, in0, s1, op0, s2=None, op1=None):
    if op1 is None:
        return lambda e: e.tensor_scalar(out=out, in0=in0, scalar1=s1, scalar2=None, op0=op0)
    return lambda e: e.tensor_scalar(out=out, in0=in0, scalar1=s1, scalar2=s2, op0=op0, op1=op1)


def STT(out, in0, scalar, in1, op0, op1):
    return lambda e: e.scalar_tensor_tensor(out=out, in0=in0, scalar=scalar, in1=in1, op0=op0, op1=op1)


def CP(out, in_):
    return lambda e: e.tensor_copy(out=out, in_=in_)


def MSET(ap, v):
    return lambda e: e.memset(ap, v)


def DMA(out, in_):
    return lambda e: e.dma_start(out=out, in_=in_)


QKV_ORDER = [("q", 0), ("k", 0), ("q", 1), ("k", 1), ("q", 2), ("k", 2), ("q", 3), ("k", 3),
             ("q", 4), ("v", 0), ("v", 1), ("q", 5), ("v", 2), ("v", 3), ("q", 6), ("q", 7)]


def stream_entries():
    ents = []
    for m in range(4):
        ents.append(("win%d" % m, 8, 128, "pre", None))
    ents.append(("wkr", 8, 64, "pre64", (32, 48)))
    for fc in range(2):
        ents.append(("wc%d" % fc, 8, 128, "pre", None))
        ents.append(("wx%d" % fc, 8, 128, "pre", None))
        ents.append(("wb%d" % fc, 8, 128, "pre", None))
    for step in QKV_ORDER:
        if step[0] == "q":
            ents.append(("wuq%d" % step[1], 2, 128, "q", (96, 112)))
        elif step[0] == "k":
            ents.append(("wuk%d" % step[1], 2, 128, "kv", None))
        elif step == ("v", 0):
            ents.append(("wuv", 2, 512, "kv512", None))
    for fc in range(2, 4):
        ents.append(("wc%d" % fc, 8, 128, "pre", None))
        ents.append(("wx%d" % fc, 8, 128, "pre", None))
        ents.append(("wb%d" % fc, 8, 128, "pre", None))
    for m in range(8):
        ents.append(("wout%d" % m, 8, 128, "out", None))
    for hf in range(2):
        for jj in range(11):
            ents.append(("wg%d" % (hf * 11 + jj), 8, 128, None, None))
            ents.append(("wu%d" % (hf * 11 + jj), 8, 128, None, None))
        for m in range(8):
            ents.append(("wd%d_%d" % (hf, m), 11, 128, None, None))
    ents.append(("wpp", 2, 1024, None, None))
    for m in range(8):
        ents.append(("wpg%d" % m, 8, 128, None, None))
    return ents


ENTS = stream_entries()
ENT_OFF = {}
_o = 0
for _e in ENTS:
    ENT_OFF[_e[0]] = _o
    _o += _e[1] * _e[2]
TOT = _o
GTYPES = {"pre": (8, 128), "pre64": (8, 64), "q": (2, 128), "kv": (2, 128), "kv512": (2, 512),
          "out": (8, 128), "ffn": (8, 128)}
G_OFF = {}
_o = 0
for _k, (_a, _b) in GTYPES.items():
    G_OFF[_k] = _o
    _o += _a * _b
TOTG = _o
N1 = [e[0] for e in ENTS].index("wg0")

C_IDENT = 0
C_MASK = 128
C_GPOST = 256
C_GFFN = 264
C_GPLE = 272
C_CONVW = 280
C_INVF = 292
C_PHASE = 293
C_GFFNPRE = 294
NCST = 302


def build_nc(nseq=NSEQ, nch=NCH, stop=99):
    nc = bass.Bass("TRN2", target_bir_lowering=False)
    x_d = nc.dram_tensor("x", [NSEQ, S, D], F32, kind="ExternalInput").ap()
    p_d = nc.dram_tensor("p", [NSEQ, S, PLE], F32, kind="ExternalInput").ap()
    pos_d = nc.dram_tensor("pos", [NSEQ, S], I32, kind="ExternalInput").ap()
    wsrc_d = nc.dram_tensor("wsrc", [128, TOT], F32, kind="ExternalInput").ap()
    gsrc_d = nc.dram_tensor("gsrc", [128, TOTG], F32, kind="ExternalInput").ap()
    cst_d = nc.dram_tensor("cst", [128, NCST], F32, kind="ExternalInput").ap()
    out_d = nc.dram_tensor("out", [NSEQ, S, D], F32, kind="ExternalOutput").ap()
    wscr_d = nc.dram_tensor("wscr", [128, TOT], BF16, kind="Internal").ap()

    Sd = Sched()
    es = ExitStack()
    with es:
        def sb(name, shape, dt):
            return es.enter_context(nc.sbuf_tensor(name, shape, dt))

        def ps(name, shape, dt):
            return es.enter_context(nc.psum_tensor(name, shape, dt))

        cst = sb("cst_sb", [128, NCST], F32)
        t_cst = T("cst")
        Sd.dma("sp", DMA(cst[:], cst_d), writes=[t_cst])
        ident = cst[:, C_IDENT:C_IDENT + 128]
        ones_bf = sb("ones_bf", [128, 128], BF16)
        mask_bf = sb("mask_bf", [128, 128], BF16)
        t_ones = T("ones")
        t_mask = T("mask")
        Sd.op("dve", MSET(ones_bf[:], 1.0), writes=[t_ones])
        Sd.op("dve", CP(mask_bf[:], cst[:, C_MASK:C_MASK + 128]), reads=[t_cst], writes=[t_mask])

        t_scr = T("wscr")
        with ExitStack() as pes:
            gt = pes.enter_context(nc.sbuf_tensor("gt", [128, TOTG], F32))
            t_gt = T("gt")
            Sd.dma("sp", DMA(gt[:], gsrc_d), writes=[t_gt])
            NST = 3
            stf = [pes.enter_context(nc.sbuf_tensor("stf%d" % i, [128, 2048], F32)) for i in range(NST)]
            stb = [pes.enter_context(nc.sbuf_tensor("stb%d" % i, [128, 2048], BF16)) for i in range(NST)]
            t_stf = [T("stf%d" % i) for i in range(NST)]
            t_stb = [T("stb%d" % i) for i in range(NST)]
            geng = ["dve", "pool"]
            gi = 0
            for ei, (name, nk, ncols, gtype, neg) in enumerate(ENTS[:N1]):
                sz = nk * ncols
                off = ENT_OFF[name]
                i = ei % NST
                Sd.dma("sp", DMA(stf[i][:, 0:sz], wsrc_d[:, off:off + sz]), writes=[t_stf[i]])
                if gtype is None:
                    Sd.op("act", ACTV(stb[i][:, 0:sz], stf[i][:, 0:sz], AF.Copy),
                          reads=[t_stf[i]], writes=[t_stb[i]])
                else:
                    go = G_OFF[gtype]
                    eng = geng[gi % 2] if sz <= 1024 else "dve"
                    gi += 1
                    Sd.op(eng, TT(stb[i][:, 0:sz], stf[i][:, 0:sz], gt[:, go:go + sz], ALU.mult),
                          reads=[t_stf[i], t_gt], writes=[t_stb[i]])
                if neg is not None:
                    v = stb[i][:, 0:sz].rearrange("p (k c) -> p k c", k=nk)[:, :, neg[0]:neg[1]]
                    Sd.op("act", ACTV(v, v, AF.Copy, scale=-1.0), reads=[t_stb[i]], writes=[t_stb[i]])
                Sd.dma("act", DMA(wscr_d[:, off:off + sz], stb[i][:, 0:sz]), reads=[t_stb[i]], writes=[t_scr])
        Sd.barrier()

        RING = 10240
        ring = sb("ring", [128, RING], BF16)
        xin = [sb("xin%d" % i, [128, D], F32) for i in range(2)]
        xin_rot = Rot(xin, "xin")
        xout_rot = Rot([sb("xout%d" % i, [128, D], F32) for i in range(2)], "xout")
        xTs = [sb("xT%d" % i, [128, 8, CH], F32) for i in range(2)]
        t_xTs = [[T("xT%d_%d" % (j, i)) for i in range(8)] for j in range(2)]
        hT = sb("hT", [128, 8, CH], BF16)
        t_hT = [T("hT%d" % i) for i in range(8)]
        yT = sb("yT", [128, 8, CH], BF16)
        t_yT = [T("yT%d" % i) for i in range(8)]
        sq_rot = Rot([sb("sq%d" % i, [128, CH], BF16) for i in range(3)], "sq")
        rbc_x = sb("rbc_x", [128, CH], F32)
        rbc_q = sb("rbc_q", [128, CH], F32)
        rbc_kv = sb("rbc_kv", [128, CH], F32)
        rbc_o = sb("rbc_o", [128, CH], F32)
        t_rbc_x, t_rbc_q, t_rbc_kv, t_rbc_o = T("rbx"), T("rbq"), T("rbkv"), T("rbo")
        lnt = sb("lnt", [128, CH], F32)
        t_lnt = T("lnt")
        cqT = sb("cqT", [128, 2, CH], BF16)
        ckvT = sb("ckvT", [128, 2, CH], BF16)
        t_cq = [T("cq0"), T("cq1")]
        t_ckv = [T("ckv0"), T("ckv1")]
        uext_rot = Rot([sb("uext%d" % i, [128, CH + 2], F32) for i in range(2)], "uext")
        halo = sb("halo", [128, 4, 2], F32)
        t_halo = [T("halo%d" % i) for i in range(4)]
        tmpf_rot = Rot([sb("tmpf%d" % i, [128, CH], F32) for i in range(5)], "tmpf")
        QT = sb("QT", [128, NH, CH], BF16)
        t_QT = [T("QT%d" % h) for h in range(NH)]
        KT = sb("KT", [128, NH, S], BF16)
        t_KT = [[T("KT%d_%d" % (h, c)) for c in range(NCH)] for h in range(NH)]
        VA = sb("VA", [128, S // 128, 4, 192], BF16)
        t_V = [T("V%d" % c) for c in range(NCH)]
        PT_rot = Rot([sb("PT%d" % i, [128, CH], BF16) for i in range(3)], "PT")
        ROT = sb("ROT_sb", [128, CH], F32)
        t_ROT = T("ROT")
        posi = sb("posi", [128, CH], I32)
        t_posi = T("posi")
        actb = sb("actb", [128, 11, CH], BF16)
        t_act = [T("act%d" % i) for i in range(11)]
        sil_rot = Rot([sb("sil%d" % i, [128, CH], BF16) for i in range(2)], "sil")
        pin_rot = Rot([sb("pin%d" % i, [128, PLE], F32) for i in range(4)], "pin")
        pT = sb("pT", [128, 2, CH], BF16)
        t_pT = T("pT")
        t_out = T("out")

        PS = [ps("ps%d" % i, [128, 512], F32) for i in range(8)]
        ps_rot = Rot(PS[0:5], "psr")
        pso_rot = Rot(PS[5:7], "pso")
        PSN = PS[7]
        t_PSN = T("psn")

        Sd.op("pool", MSET(QT[96:128, :, :], 0.0), writes=t_QT)
        Sd.op("pool", MSET(KT[96:128, :, :], 0.0), writes=[t for row in t_KT for t in row])
        for c in range(NCH):
            Sd.op("pool", MSET(VA[:, c * 4:(c + 1) * 4, :, 64:128], 1.0), writes=[t_V[c]])

        p2 = list(ENTS[N1:])
        t_scr_e = {e[0]: T("scr_" + e[0]) for e in p2}
        p2_stored = set()
        p2_k = [0]

        def pump(n):
            lab = Sd.label
            Sd.label = "prep2"
            for _ in range(n):
                k = p2_k[0]
                if k >= len(p2):
                    break
                name, nk, ncols, gtype, neg = p2[k]
                sz, off = nk * ncols, ENT_OFF[name]
                Sd.dma("pool", DMA(wscr_d[:, off:off + sz], wsrc_d[:, off:off + sz]), writes=[t_scr_e[name]],
                       own_sem="d2_%d" % k)
                p2_stored.add(name)
                p2_k[0] += 1
            Sd.label = lab

        class Ring:
            def __init__(self):
                self.live = []
                self.head = 0
                self.issued = 0
                self.got = 0
                self.slots = {}

            def _try_issue(self):
                idx = self.issued
                name, nk, ncols, _, _ = ENTS[idx % len(ENTS)]
                if name in t_scr_e and name not in p2_stored:
                    return False
                sz = nk * ncols
                start = self.head
                if start + sz > RING:
                    start = 0
                end = start + sz
                ov = [l for l in self.live if not (l[1] <= start or l[0] >= end)]
                for l in ov:
                    if l[3] >= self.got - 1:
                        return False
                extra = []
                for l in ov:
                    extra += l[2].readers
                    if l[2].last_w is not None:
                        extra.append(l[2].last_w)
                    self.live.remove(l)
                t = T("w%d" % idx)
                off = ENT_OFF[name]
                Sd.dma("sp", DMA(ring[:, start:end], wscr_d[:, off:off + sz]), reads=[t_scr_e.get(name, t_scr)],
                       writes=[t], extra=extra)
                self.live.append((start, end, t, idx))
                self.slots[idx] = (start, t, nk, ncols)
                self.head = end
                self.issued += 1
                return True

            def fill(self, total):
                while self.issued < total and self.issued < self.got + 40:
                    if not self._try_issue():
                        break

            def get(self, name, total):
                idx = self.got
                assert ENTS[idx % len(ENTS)][0] == name, (ENTS[idx % len(ENTS)][0], name)
                self.fill(total)
                assert idx in self.slots, "ring too small for " + name
                start, t, nk, ncols = self.slots.pop(idx)
                self.got += 1
                v = ring[:, start:start + nk * ncols].rearrange("p (k c) -> p k c", k=nk)
                return v, t

        R = Ring()
        TOTAL_ENT = nseq * nch * len(ENTS)

        def getw(name):
            return R.get(name, TOTAL_ENT)

        def sumsq_bcast(srcs, dst_rbc, t_dst, n, extra_scale=1.0, keep_sq=None, sq_eng="act"):
            nsrc = len(srcs)
            pend_mm = []

            def mk(i, sqb, t_sq):
                return lambda: Sd.op("pe", MM(PSN[:], ones_bf[:], sqb, i == 0, i == nsrc - 1), reads=[t_sq, t_ones],
                                     writes=[t_PSN])
            for i, (ap, ts) in enumerate(srcs):
                if keep_sq is not None:
                    sqb, t_sq = keep_sq[i]
                else:
                    sqb, t_sq = sq_rot.next()
                    sqb = sqb[:]
                if sq_eng == "act":
                    Sd.op("act", ACTV(sqb, ap, AF.Square), reads=ts, writes=[t_sq])
                else:
                    Sd.op("dve", TT(sqb, ap, ap, ALU.mult), reads=ts, writes=[t_sq])
                pend_mm.append(mk(i, sqb, t_sq))
                if len(pend_mm) > 2:
                    pend_mm.pop(0)()
            for f_ in pend_mm:
                f_()
            Sd.op("act", ACTV(lnt[:], PSN[:], AF.Ln, bias=EPS, scale=1.0 / n), reads=[t_PSN], writes=[t_lnt])
            Sd.op("act", ACTV(dst_rbc, lnt[:], AF.Exp, scale=-0.5),
                  reads=[t_lnt], writes=[t_dst])

        evac_flip = [0]

        def evac(out, in_, reads, writes):
            evac_flip[0] ^= 1
            if evac_flip[0]:
                Sd.op("act", ACTV(out, in_, AF.Copy), reads=reads, writes=writes)
            else:
                Sd.op("dve", CP(out, in_), reads=reads, writes=writes)

        xpre = {}

        def eng8(i):
            return "dve"

        def rope_table(s, c):
            Sd.label = "rope"
            tok0 = c * CH
            Sd.dma("sp", DMA(posi[:], pos_d[s, tok0:tok0 + CH].partition_broadcast(128)), writes=[t_posi])
            a0, ta0 = tmpf_rot.next()
            a1, ta1 = tmpf_rot.next()
            a2, ta2 = tmpf_rot.next()
            Sd.op("pool", CP(a0[:], posi[:]), reads=[t_posi], writes=[ta0])
            Sd.op("pool", TS(a0[:], a0[:], cst[:, C_INVF:C_INVF + 1], ALU.mult,
                             cst[:, C_PHASE:C_PHASE + 1], ALU.add), reads=[ta0, t_cst], writes=[ta0])
            Sd.op("pool", TS(a1[:], a0[:], 1.0 / (2 * math.pi), ALU.mult, 0.0, ALU.add), reads=[ta0], writes=[ta1])
            Sd.op("pool", CP(posi[:], a1[:]), reads=[ta1], writes=[t_posi])
            Sd.op("pool", CP(a1[:], posi[:]), reads=[t_posi], writes=[ta1])
            Sd.op("dve", STT(a0[:], a1[:], -2 * math.pi, a0[:], ALU.mult, ALU.add), reads=[ta0, ta1], writes=[ta0])
            Sd.op("dve", TS(a2[:], a0[:], math.pi, ALU.is_gt, -2 * math.pi, ALU.mult), reads=[ta0], writes=[ta2])
            Sd.op("dve", TT(a0[:], a0[:], a2[:], ALU.add), reads=[ta0, ta2], writes=[ta0])
            Sd.op("dve", TS(a2[:], a0[:], -math.pi, ALU.is_lt, 2 * math.pi, ALU.mult), reads=[ta0], writes=[ta2])
            Sd.op("dve", TT(a0[:], a0[:], a2[:], ALU.add), reads=[ta0, ta2], writes=[ta0])
            Sd.op("act", ACTV(ROT[:], a0[:], AF.Sin), reads=[ta0], writes=[t_ROT])

        def x_tiles(s, c, par, tts, act_only=False):
            Sd.label = "xload"
            for tt in tts:
                if (s, c, tt) in xpre:
                    xb, t_xb = xpre.pop((s, c, tt))
                else:
                    xb, t_xb = xin_rot.next()
                    Sd.dma("sp", DMA(xb[:], x_d[s, c * CH + tt * 128:c * CH + (tt + 1) * 128, :]), writes=[t_xb])
                for half in range(2):
                    pb, t_pb = ps_rot.next()
                    for q4 in range(4):
                        fc = half * 4 + q4
                        Sd.op("pe", TR(pb[:, q4 * 128:(q4 + 1) * 128], xb[:, fc * 128:(fc + 1) * 128], ident),
                              reads=[t_xb, t_cst], writes=[t_pb])
                    if act_only:
                        Sd.op("act", ACTV(xTs[par][:, half * 4:half * 4 + 4, tt * 128:(tt + 1) * 128],
                                          pb[:, :].rearrange("p (a b) -> p a b", a=4), AF.Copy), reads=[t_pb],
                              writes=t_xTs[par][half * 4:half * 4 + 4])
                    else:
                        evac(xTs[par][:, half * 4:half * 4 + 4, tt * 128:(tt + 1) * 128],
                             pb[:, :].rearrange("p (a b) -> p a b", a=4), [t_pb], t_xTs[par][half * 4:half * 4 + 4])

        def x_prefetch(s, c, tts):
            for tt in tts:
                xb, t_xb = xin_rot.next()
                Sd.dma("sp", DMA(xb[:], x_d[s, c * CH + tt * 128:c * CH + (tt + 1) * 128, :]), writes=[t_xb])
                xpre[(s, c, tt)] = (xb, t_xb)

        def chunk_after(s, c):
            return (s, c + 1) if c + 1 < nch else ((s + 1, 0) if s + 1 < nseq else None)

        def front_A(s, c, par, act_only=False):
            x_tiles(s, c, par, (0, 1), act_only)
            x_prefetch(s, c, (2, 3))

        def front_B(s, c, par, act_only=False):
            x_tiles(s, c, par, (2, 3), act_only)
            n2 = chunk_after(s, c)
            if n2 is not None:
                x_prefetch(n2[0], n2[1], (0, 1))
            Sd.label = "prenorm"
            sumsq_bcast([(xTs[par][:, fc, :], [t_xTs[par][fc]]) for fc in range(8)], rbc_kv[:], t_rbc_kv, D)

        front_A(0, 0, 0)
        front_B(0, 0, 0)
        rope_table(0, 0)
        h_ready = [False]
        for s in range(nseq):
            for c in range(nch):
                tok0 = c * CH
                first_chunk = (s == 0 and c == 0)

                def P(n):
                    if first_chunk:
                        pump(n)

                par = (s * nch + c) % 2
                xT = xTs[par]
                t_xT = t_xTs[par]
                nxt = chunk_after(s, c)
                pins = []
                for tt in range(4):
                    pbuf, t_pbuf = pin_rot.next()
                    Sd.dma("act", DMA(pbuf[:], p_d[s, tok0 + tt * 128:tok0 + (tt + 1) * 128, :]), writes=[t_pbuf])
                    pins.append((pbuf, t_pbuf))
                if stop <= 2:
                    continue
                Sd.label = "prenorm"
                if not h_ready[0]:
                    for fc in range(8):
                        Sd.op("dve", TT(hT[:, fc, :], xT[:, fc, :], rbc_kv[:], ALU.mult),
                              reads=[t_xT[fc], t_rbc_kv], writes=[t_hT[fc]])
                h_ready[0] = False

                if stop <= 3:
                    continue
                Sd.label = "latents"
                def proj8(wname, rhs_t=hT, rhs_ts=t_hT, M=128):
                    w, t_w = getw(wname)
                    pb, t_pb = ps_rot.next()
                    for kc in range(8):
                        Sd.op("pe", MM(pb[0:M, :], w[:, kc, :], rhs_t[:, kc, :], kc == 0, kc == 7),
                              reads=[t_w, rhs_ts[kc]], writes=[t_pb])
                    return pb, t_pb

                def evac_lat(pb, t_pb):
                    rw, t_rw = tmpf_rot.next()
                    Sd.op("act", ACTV(rw[:], pb[:], AF.Copy), reads=[t_pb], writes=[t_rw])
                    sqb, t_sq = sq_rot.next()
                    Sd.op("act", ACTV(sqb[:], pb[:], AF.Square), reads=[t_pb], writes=[t_sq])
                    return rw, t_rw, sqb, t_sq

                def lat_sum(e, first, last):
                    Sd.op("pe", MM(PSN[:], ones_bf[:], e[2][:], first, last), reads=[e[3], t_ones], writes=[t_PSN])

                e0 = evac_lat(*proj8("win0"))
                e1 = evac_lat(*proj8("win1"))
                pb2 = proj8("win2")
                lat_sum(e0, True, False)
                lat_sum(e1, False, True)
                e2 = evac_lat(*pb2)
                Sd.op("act", ACTV(lnt[:], PSN[:], AF.Ln, bias=EPS, scale=1.0 / 256), reads=[t_PSN], writes=[t_lnt])
                Sd.op("act", ACTV(rbc_q[:], lnt[:], AF.Exp, scale=-0.5), reads=[t_lnt], writes=[t_rbc_q])
                pb3 = proj8("win3")
                for i, e in enumerate((e0, e1)):
                    Sd.op("pool", TT(cqT[:, i, :], e[0][:], rbc_q[:], ALU.mult),
                          reads=[e[1], t_rbc_q], writes=[t_cq[i]])
                e3 = evac_lat(*pb3)

                def finish_kv():
                    lat_sum(e2, True, False)
                    lat_sum(e3, False, True)
                    Sd.op("act", ACTV(lnt[:], PSN[:], AF.Ln, bias=EPS, scale=1.0 / 256), reads=[t_PSN], writes=[t_lnt])
                    Sd.op("act", ACTV(rbc_kv[:], lnt[:], AF.Exp, scale=-0.5), reads=[t_lnt], writes=[t_rbc_kv])
                    for i, e in enumerate((e2, e3)):
                        Sd.op("pool", TT(ckvT[:, i, :], e[0][:], rbc_kv[:], ALU.mult),
                              reads=[e[1], t_rbc_kv], writes=[t_ckv[i]])

                P(6)
                if stop <= 4:
                    continue
                Sd.label = "krope"
                pb, t_pb = proj8("wkr", M=64)
                Sd.label = "latents"
                finish_kv()
                Sd.label = "krope"
                t1, tt1 = tmpf_rot.next()
                t2, tt2 = tmpf_rot.next()
                Sd.op("dve", TT(t1[64:96, :], pb[0:32, :], ROT[0:32, :], ALU.mult), reads=[t_pb, t_ROT], writes=[tt1])
                Sd.op("dve", TT(t2[64:96, :], pb[32:64, :], ROT[32:64, :], ALU.mult), reads=[t_pb, t_ROT], writes=[tt2])
                for h in range(NH):
                    Sd.op("pool", TT(KT[64:96, h, tok0:tok0 + CH], t1[64:96, :], t2[64:96, :], ALU.add),
                          reads=[tt1, tt2], writes=[t_KT[h][c]])

                if stop <= 6:
                    continue
                def conv_fc(fc):
                    Sd.label = "conv"
                    pc, t_pc = proj8("wc%d" % fc)
                    px, t_px = proj8("wx%d" % fc)
                    pbb, t_pbb = proj8("wb%d" % fc)
                    csb, t_csb = tmpf_rot.next()
                    acc, t_acc = tmpf_rot.next()
                    Sd.op("act", ACTV(csb[:], pc[:], AF.Copy), reads=[t_pc], writes=[t_csb])
                    ue, t_ue = uext_rot.next()
                    if c == 0:
                        Sd.op("pool", MSET(ue[:, 0:2], 0.0), writes=[t_ue])
                    else:
                        Sd.op("pool", CP(ue[:, 0:2], halo[:, fc, :]), reads=[t_halo[fc]], writes=[t_ue])
                    Sd.op("dve", TT(ue[:, 2:CH + 2], px[:], csb[:], ALU.mult), reads=[t_px, t_csb],
                          writes=[t_ue])
                    Sd.op("pool", CP(halo[:, fc, :], ue[:, CH:CH + 2]), reads=[t_ue], writes=[t_halo[fc]])
                    cw = lambda k: cst[:, C_CONVW + k * 4 + fc:C_CONVW + k * 4 + fc + 1]
                    Sd.op("pool", TS(acc[:], ue[:, 2:CH + 2], cw(0), ALU.mult, 0.0, ALU.add), reads=[t_ue, t_cst],
                          writes=[t_acc])
                    Sd.op("dve", STT(acc[:], ue[:, 1:CH + 1], cw(1), acc[:], ALU.mult, ALU.add),
                          reads=[t_ue, t_acc, t_cst], writes=[t_acc])
                    Sd.op("dve", STT(acc[:], ue[:, 0:CH], cw(2), acc[:], ALU.mult, ALU.add),
                          reads=[t_ue, t_acc, t_cst], writes=[t_acc])
                    Sd.op("dve", TT(yT[:, 4 + fc, :], pbb[:], acc[:], ALU.mult), reads=[t_pbb, t_acc],
                          writes=[t_yT[4 + fc]])
                    Sd.label = "attn"

                conv_fc(0)
                conv_fc(1)
                P(6)
                def q_head(h):
                    Sd.label = "qproj"
                    w, t_w = getw("wuq%d" % h)
                    pb, t_pb = ps_rot.next()
                    for kc in range(2):
                        Sd.op("pe", MM(pb[:], w[:, kc, :], cqT[:, kc, :], kc == 0, kc == 1),
                              reads=[t_w, t_cq[kc]], writes=[t_pb])
                    t1, tt1 = tmpf_rot.next()
                    t2, tt2 = tmpf_rot.next()
                    Sd.op("dve", TT(t1[64:96, :], pb[64:96, :], ROT[64:96, :], ALU.mult), reads=[t_pb, t_ROT],
                          writes=[tt1])
                    ev2 = Sd.op("dve", TT(t2[64:96, :], pb[96:128, :], ROT[96:128, :], ALU.mult), reads=[t_pb, t_ROT],
                                writes=[tt2])
                    Sd.op("act", ACTV(QT[0:64, h, :], pb[0:64, :], AF.Copy), reads=[t_pb], writes=[t_QT[h]], extra=[ev2])
                    Sd.op("pool", TT(QT[64:96, h, :], t1[64:96, :], t2[64:96, :], ALU.add), reads=[tt1, tt2],
                          writes=[t_QT[h]])

                def k_pair(hp):
                    Sd.label = "kproj"
                    w, t_w = getw("wuk%d" % hp)
                    pb, t_pb = ps_rot.next()
                    for kc in range(2):
                        Sd.op("pe", MM(pb[:], w[:, kc, :], ckvT[:, kc, :], kc == 0, kc == 1),
                              reads=[t_w, t_ckv[kc]], writes=[t_pb])
                    Sd.op("act", ACTV(KT[0:64, 2 * hp, tok0:tok0 + CH], pb[0:64, :], AF.Copy),
                          reads=[t_pb], writes=[t_KT[2 * hp][c]])
                    Sd.op("act", ACTV(KT[0:64, 2 * hp + 1, tok0:tok0 + CH], pb[64:128, :], AF.Copy),
                          reads=[t_pb], writes=[t_KT[2 * hp + 1][c]])

                wv_ = [None]

                def v_tile(tt):
                    Sd.label = "vproj"
                    if wv_[0] is None:
                        wv_[0] = getw("wuv")
                    w, t_w = wv_[0]
                    pb, t_pb = ps_rot.next()
                    for kc in range(2):
                        Sd.op("pe", MM(pb[:], ckvT[:, kc, tt * 128:(tt + 1) * 128], w[:, kc, :], kc == 0, kc == 1),
                              reads=[t_w, t_ckv[kc]], writes=[t_pb])
                    pv4 = pb[:, :].rearrange("p (h b d) -> p h b d", h=4, b=2)
                    for b_ in range(2):
                        Sd.op("act", ACTV(VA[:, c * 4 + tt, :, b_ * 128:b_ * 128 + 64], pv4[:, :, b_, :], AF.Copy),
                              reads=[t_pb], writes=[t_V[c]])

                for step in QKV_ORDER:
                    if step[0] == "q":
                        q_head(step[1])
                    elif step[0] == "k":
                        k_pair(step[1])
                    else:
                        v_tile(step[1])
                P(12)
                P(3)
                if stop <= 9:
                    continue
                Sd.label = "attn"
                nkt = 4 * c + 4
                LA = 2
                seq = [(h, kt) for h in range(NH) for kt in range(nkt)]
                pos = {}

                def score(h, kt):
                    j = kt - 4 * c
                    q0 = max(j, 0) * 128
                    pb, t_pb = ps_rot.next()
                    Sd.op("pe", MM(pb[:, q0:CH], KT[:, h, kt * 128:(kt + 1) * 128], QT[:, h, q0:CH],
                                   True, True), reads=[t_KT[h][kt // 4], t_QT[h]], writes=[t_pb])
                    return pb, t_pb, q0, j

                def expo(sc):
                    pb, t_pb, q0, j = sc
                    ptb, t_pt = PT_rot.next()
                    Sd.op("act", ACTV(ptb[:, q0:CH], pb[:, q0:CH], AF.Exp, scale=QSCALE), reads=[t_pb], writes=[t_pt])
                    if j >= 0:
                        Sd.op("pool", TT(ptb[:, q0:q0 + 128], ptb[:, q0:q0 + 128], mask_bf[:], ALU.mult),
                              reads=[t_pt, t_mask], writes=[t_pt])
                    return ptb, t_pt, q0

                def pv(h, kt, pt):
                    ptb, t_pt, q0 = pt
                    po, t_po = pos[h]
                    Sd.op("pe", MM(po[:, q0:CH], VA[:, kt, h // 2, (h % 2) * 64:(h % 2) * 64 + 128], ptb[:, q0:CH],
                                   kt == 0, kt == nkt - 1),
                          reads=[t_pt, t_V[kt // 4]], writes=[t_po])

                def epilogue(h):
                    po, t_po = pos.pop(h)
                    l1, tl1 = tmpf_rot.next()
                    hp_, ho = h // 2, (h % 2) * 64
                    Sd.op("act", ACTV(l1[ho:ho + 64, :], po[64 - ho:128 - ho, :], AF.Ln), reads=[t_po], writes=[tl1])
                    Sd.op("act", ACTV(l1[ho:ho + 64, :], l1[ho:ho + 64, :], AF.Exp, scale=-1.0), reads=[tl1], writes=[tl1])
                    Sd.op("dve", TT(yT[ho:ho + 64, hp_, :], po[ho:ho + 64, :], l1[ho:ho + 64, :], ALU.mult),
                          reads=[t_po, tl1], writes=[t_yT[hp_]])
                    P(6)
                    if h in (0, 2):
                        conv_fc(h // 2 + 2)
                        if h == 2:
                            Sd.label = "gnorm"
                            sumsq_bcast([(yT[:, 4 + i, :], [t_yT[4 + i]]) for i in range(4)], rbc_o[:], t_rbc_o, 512,
                                        sq_eng="dve")
                            for i in range(4):
                                Sd.op("dve" if i % 2 == 0 else "pool", TT(hT[:, 4 + i, :], yT[:, 4 + i, :], rbc_o[:], ALU.mult),
                                      reads=[t_yT[4 + i], t_rbc_o], writes=[t_hT[4 + i]])
                            Sd.label = "attn"

                nseqq = len(seq)
                conv_iters = [(hh + 1) * nkt for hh in (0, 2)]

                def target(i):
                    t = i + 3
                    for ci in conv_iters:
                        if ci >= i + 1 or ci == i:
                            pass
                    nxt_ci = [ci for ci in conv_iters if ci > i]
                    if nxt_ci:
                        t = min(t, nxt_ci[0] + 1)
                    if i in conv_iters:
                        t = min(t, i + 2)
                    return min(t, nseqq - 1)

                scs = {}
                issued = 0
                t0_ = min(2, nseqq - 1, conv_iters[0] + 1)
                while issued <= t0_:
                    scs[issued] = score(*seq[issued])
                    issued += 1
                pend_ep = None
                for i in range(nseqq):
                    h, kt = seq[i]
                    if kt == 0:
                        pos[h] = pso_rot.next()
                    pt = expo(scs.pop(i))
                    if pend_ep is not None:
                        epilogue(pend_ep)
                        pend_ep = None
                    tg = target(i)
                    while issued <= tg:
                        scs[issued] = score(*seq[issued])
                        issued += 1
                    pv(h, kt, pt)
                    if kt == nkt - 1:
                        pend_ep = h
                epilogue(pend_ep)

                if first_chunk:
                    pump(len(p2) + 2)
                if stop <= 10:
                    continue
                Sd.label = "gnorm"
                sumsq_bcast([(yT[:, i, :], [t_yT[i]]) for i in range(4)], rbc_o[:], t_rbc_o, 512)
                for i in range(4):
                    Sd.op("dve" if i != 3 else "pool", TT(hT[:, i, :], yT[:, i, :], rbc_o[:], ALU.mult),
                          reads=[t_yT[i], t_rbc_o], writes=[t_hT[i]])

                if stop <= 11:
                    continue
                Sd.label = "wout"
                def sq_and_sum(src_ap, src_ts, acc_ps, t_acc, first, last):
                    sqb, t_sq = sq_rot.next()
                    Sd.op("act", ACTV(sqb[:], src_ap, AF.Square), reads=src_ts, writes=[t_sq])
                    return lambda: Sd.op("pe", MM(acc_ps[:], ones_bf[:], sqb[:], first, last), reads=[t_sq, t_ones],
                                         writes=[t_acc])

                def rstd_from(acc_ps, t_acc, dst, t_dst, n):
                    Sd.op("act", ACTV(lnt[:], acc_ps[:], AF.Ln, bias=EPS, scale=1.0 / n), reads=[t_acc], writes=[t_lnt])
                    Sd.op("act", ACTV(dst[:], lnt[:], AF.Exp, scale=-0.5), reads=[t_lnt], writes=[t_dst])

                def residual_add(next_norm):
                    pend = None
                    for m in range(8):
                        e_ = eng8(m)
                        tb, t_tb = tmpf_rot.next()
                        Sd.op(e_, TT(tb[:], yT[:, m, :], rbc_o[:], ALU.mult), reads=[t_yT[m], t_rbc_o], writes=[t_tb])
                        Sd.op(e_, TT(xT[:, m, :], xT[:, m, :], tb[:], ALU.add), reads=[t_tb, t_xT[m]], writes=[t_xT[m]])
                        if next_norm:
                            nxt_ = sq_and_sum(xT[:, m, :], [t_xT[m]], PSN, t_PSN, m == 0, m == 7)
                            if pend is not None:
                                pend()
                            pend = nxt_
                    if pend is not None:
                        pend()

                def gcol(base, m):
                    return cst[:, base + m:base + m + 1]

                pend = None
                for m in range(8):
                    pb, t_pb = proj8("wout%d" % m)
                    if pend is not None:
                        pend()
                    pend = sq_and_sum(pb[:], [t_pb], PSN, t_PSN, m == 0, m == 7)
                    Sd.op("act", ACTV(yT[:, m, :], pb[:], AF.Identity, scale=gcol(C_GPOST, m)), reads=[t_pb, t_cst],
                          writes=[t_yT[m]])
                pend()
                rstd_from(PSN, t_PSN, rbc_o, t_rbc_o, D)
                if nxt is not None:
                    front_A(nxt[0], nxt[1], 1 - par, act_only=True)
                    Sd.label = "wout"
                residual_add(True)

                if stop <= 12:
                    continue
                Sd.label = "ffn"
                def xg_op(fc):
                    xg, t_xg = tmpf_rot.next()
                    Sd.op("act", ACTV(xg[:], xT[:, fc, :], AF.Identity, scale=gcol(C_GFFNPRE, fc)),
                          reads=[t_xT[fc], t_cst], writes=[t_xg])
                    return xg, t_xg
                xgs = [xg_op(fc) for fc in range(3)]
                rstd_from(PSN, t_PSN, rbc_x, t_rbc_x, D)
                for fc in range(8):
                    xg, t_xg = xgs[fc]
                    Sd.op("dve", TT(hT[:, fc, :], xg[:], rbc_x[:], ALU.mult),
                          reads=[t_xg, t_rbc_x], writes=[t_hT[fc]])
                    if fc + 3 < 8:
                        xgs.append(xg_op(fc + 3))
                ffn_pend = []
                for hf in range(2):
                    for jj in range(11):
                        j = hf * 11 + jj
                        pg, t_pg = proj8("wg%d" % j)
                        pu, t_pu = proj8("wu%d" % j)
                        sl, t_sl = sil_rot.next()
                        Sd.op("act", ACTV(sl[:], pg[:], AF.Silu), reads=[t_pg], writes=[t_sl])
                        Sd.op("dve", TT(actb[:, jj, :], pu[:], sl[:], ALU.mult), reads=[t_pu, t_sl], writes=[t_act[jj]])
                    for m in range(8):
                        w, t_w = getw("wd%d_%d" % (hf, m))
                        pb, t_pb = ps_rot.next()
                        for jj in range(11):
                            Sd.op("pe", MM(pb[:], w[:, jj, :], actb[:, jj, :], jj == 0, jj == 10),
                                  reads=[t_w, t_act[jj]], writes=[t_pb])
                        if hf == 0:
                            Sd.op("act", ACTV(yT[:, m, :], pb[:], AF.Copy), reads=[t_pb], writes=[t_yT[m]])
                        else:
                            Sd.op("dve", TT(yT[:, m, :], pb[:], yT[:, m, :], ALU.add), reads=[t_pb, t_yT[m]],
                                  writes=[t_yT[m]])
                            nxt_ = sq_and_sum(yT[:, m, :], [t_yT[m]], PSN, t_PSN, m == 0, m == 7)
                            for f_ in ffn_pend:
                                f_()
                            ffn_pend = [nxt_]
                            Sd.op("act", ACTV(yT[:, m, :], yT[:, m, :], AF.Identity, scale=gcol(C_GFFN, m)),
                                  reads=[t_yT[m], t_cst], writes=[t_yT[m]])
                    if hf == 0 and nxt is not None:
                        rope_table(nxt[0], nxt[1])
                        Sd.label = "ffn"

                if stop <= 13:
                    continue
                Sd.label = "ple"
                for tt in range(4):
                    pbuf, t_pbuf = pins[tt]
                    pb, t_pb = ps_rot.next()
                    for i in range(2):
                        Sd.op("pe", TR(pb[:, i * 128:(i + 1) * 128], pbuf[:, i * 128:(i + 1) * 128], ident),
                              reads=[t_pbuf, t_cst], writes=[t_pb])
                    Sd.op("dve", CP(pT[:, :, tt * 128:(tt + 1) * 128], pb[:, 0:256].rearrange("p (a b) -> p a b", a=2)),
                          reads=[t_pb], writes=[t_pT])
                Sd.label = "ffn"
                for f_ in ffn_pend:
                    f_()
                rstd_from(PSN, t_PSN, rbc_o, t_rbc_o, D)
                Sd.label = "ple"
                w, t_w = getw("wpp")
                PSE, t_PSE = pso_rot.next()
                pend = None
                for m in range(8):
                    pb, t_pb = ps_rot.next()
                    for kc in range(2):
                        Sd.op("pe", MM(pb[:], w[:, kc, m * 128:(m + 1) * 128], pT[:, kc, :], kc == 0, kc == 1),
                              reads=[t_w, t_pT], writes=[t_pb])
                    if pend is not None:
                        pend()
                    pend = sq_and_sum(pb[:], [t_pb], PSE, t_PSE, m == 0, m == 7)
                    Sd.op("act", ACTV(actb[:, m, :], pb[:], AF.Identity, scale=gcol(C_GPLE, m)), reads=[t_pb, t_cst],
                          writes=[t_act[m]])
                pend()
                Sd.label = "ffn"
                if nxt is not None:
                    front_B(nxt[0], nxt[1], 1 - par, act_only=True)
                    Sd.label = "ffn"
                residual_add(False)
                Sd.label = "ple"
                rstd_from(PSE, t_PSE, rbc_q, t_rbc_q, D)
                for m in range(8):
                    Sd.op(eng8(m), TT(actb[:, m, :], actb[:, m, :], rbc_q[:], ALU.mult), reads=[t_act[m], t_rbc_q],
                          writes=[t_act[m]])
                for fc in range(8):
                    if fc % 2 == 0:
                        Sd.op("act", ACTV(hT[:, fc, :], xT[:, fc, :], AF.Copy), reads=[t_xT[fc]], writes=[t_hT[fc]])
                    else:
                        Sd.op(eng8(fc), CP(hT[:, fc, :], xT[:, fc, :]), reads=[t_xT[fc]], writes=[t_hT[fc]])
                for m in range(8):
                    pb, t_pb = proj8("wpg%d" % m)
                    sg, t_sg = tmpf_rot.next()
                    Sd.op("act", ACTV(sg[:], pb[:], AF.Sigmoid), reads=[t_pb], writes=[t_sg])
                    Sd.op("dve", TT(sg[:], sg[:], actb[:, m, :], ALU.mult), reads=[t_sg, t_act[m]], writes=[t_sg])
                    Sd.op(eng8(m), TT(xT[:, m, :], xT[:, m, :], sg[:], ALU.add), reads=[t_sg, t_xT[m]], writes=[t_xT[m]])

                if stop <= 14:
                    continue
                if nxt is not None:
                    Sd.label = "prenorm"
                    for fc in range(8):
                        Sd.op("dve", TT(hT[:, fc, :], xTs[1 - par][:, fc, :], rbc_kv[:], ALU.mult),
                              reads=[t_xTs[1 - par][fc], t_rbc_kv], writes=[t_hT[fc]])
                    h_ready[0] = True
                Sd.label = "store"
                for tt in range(4):
                    xo, t_xo = xout_rot.next()
                    for half in range(2):
                        pb, t_pb = ps_rot.next()
                        for q4 in range(4):
                            fc = half * 4 + q4
                            Sd.op("pe", TR(pb[:, q4 * 128:(q4 + 1) * 128], xT[:, fc, tt * 128:(tt + 1) * 128], ident),
                                  reads=[t_xT[fc], t_cst], writes=[t_pb])
                        if tt < 2:
                            Sd.op("act", ACTV(xo[:, half * 512:(half + 1) * 512], pb[:], AF.Copy), reads=[t_pb],
                                  writes=[t_xo])
                        else:
                            Sd.op("dve", CP(xo[:, half * 512:(half + 1) * 512], pb[:]), reads=[t_pb], writes=[t_xo])
                    Sd.dma("pool", DMA(out_d[s, tok0 + tt * 128:tok0 + (tt + 1) * 128, :], xo[:]), reads=[t_xo],
                           writes=[t_out])

        for ev in Sd.dma_last.values():
            if ev.eng == "dma_pool":
                Sd._wait("pool", ev)

        nc._sched_labels = Sd.labels
        keys = Sd.sem_keys()
        sems = {k: es.enter_context(nc.semaphore(k)) for k in keys}
        block = es.enter_context(nc.Block())
        Sd.emit(block, sems)
    return nc


def _ent_k(W, cols, nk):
    W = np.asarray(W)
    return np.ascontiguousarray(W.reshape(nk, 128, W.shape[1])[:, :, cols].transpose(1, 0, 2)).reshape(128, -1)


def _gain_k(g, nk, ncols):
    g = np.asarray(g, dtype=np.float32).reshape(nk, 128)
    return np.ascontiguousarray(np.broadcast_to(g.T[:, :, None], (128, nk, ncols))).reshape(128, -1)


def host_layout(inp):
    w_in = inp["w_in"][0]
    w_uq = inp["w_uq"][0]
    w_ukv = inp["w_ukv"][0]
    w_out = inp["w_out"][0]
    w_gate = inp["w_gate"][0]
    w_up = inp["w_up"][0]
    w_down = inp["w_down"][0]
    w_pp = inp["w_ple_proj"][0]
    w_pg = inp["w_ple_gate"][0]
    ar = np.arange
    parts = {}
    for m in range(2):
        parts["win%d" % m] = _ent_k(w_in, ar(m * 128, (m + 1) * 128), 8)
        parts["win%d" % (2 + m)] = _ent_k(w_in, O1 + ar(m * 128, (m + 1) * 128), 8)
    krc = np.concatenate([O2 + ar(0, 32), O2 + ar(16, 32), O2 + ar(0, 16)])
    parts["wkr"] = _ent_k(w_in, krc, 8)
    for fc in range(4):
        parts["wb%d" % fc] = _ent_k(w_in, O3 + ar(fc * 128, (fc + 1) * 128), 8)
        parts["wc%d" % fc] = _ent_k(w_in, O4 + ar(fc * 128, (fc + 1) * 128), 8)
        parts["wx%d" % fc] = _ent_k(w_in, O5 + ar(fc * 128, (fc + 1) * 128), 8)
    for h in range(NH):
        b = h * 96
        qc = np.concatenate([b + ar(0, 96), b + ar(80, 96), b + ar(64, 80)])
        parts["wuq%d" % h] = _ent_k(w_uq, qc, 2)
    for hp in range(4):
        kc = np.concatenate([(2 * hp) * 128 + ar(0, 64), (2 * hp + 1) * 128 + ar(0, 64)])
        parts["wuk%d" % hp] = _ent_k(w_ukv, kc, 2)
    vc = np.concatenate([h * 128 + 64 + ar(0, 64) for h in range(NH)])
    parts["wuv"] = _ent_k(w_ukv, vc, 2)
    for m in range(8):
        parts["wout%d" % m] = _ent_k(w_out, ar(m * 128, (m + 1) * 128), 8)
        parts["wpg%d" % m] = _ent_k(w_pg, ar(m * 128, (m + 1) * 128), 8)
    for j in range(22):
        parts["wg%d" % j] = _ent_k(w_gate, ar(j * 128, (j + 1) * 128), 8)
        parts["wu%d" % j] = _ent_k(w_up, ar(j * 128, (j + 1) * 128), 8)
    for hf in range(2):
        for m in range(8):
            parts["wd%d_%d" % (hf, m)] = _ent_k(w_down[hf * 1408:(hf + 1) * 1408], ar(m * 128, (m + 1) * 128), 11)
    parts["wpp"] = _ent_k(w_pp, ar(0, 1024), 2)
    wsrc = np.concatenate([parts[e[0]] for e in ENTS], axis=1).astype(np.float32)
    assert wsrc.shape == (128, TOT)

    g_out = np.concatenate([inp["attn_group_norm"][0], inp["conv_group_norm"][0]])
    gsrc = np.concatenate([
        _gain_k(inp["mix_pre_norm"][0], 8, 128), _gain_k(inp["mix_pre_norm"][0], 8, 64),
        _gain_k(inp["q_norm"][0], 2, 128), _gain_k(inp["kv_norm"][0], 2, 128), _gain_k(inp["kv_norm"][0], 2, 512),
        _gain_k(g_out, 8, 128), _gain_k(inp["ffn_pre_norm"][0], 8, 128)], axis=1).astype(np.float32)
    assert gsrc.shape == (128, TOTG)

    cst = np.zeros((128, NCST), np.float32)
    cst[:, C_IDENT:C_IDENT + 128] = np.eye(128, dtype=np.float32)
    cst[:, C_MASK:C_MASK + 128] = np.triu(np.ones((128, 128), np.float32))
    cst[:, C_GPOST:C_GPOST + 8] = np.asarray(inp["mix_post_norm"][0]).reshape(8, 128).T
    cst[:, C_GFFN:C_GFFN + 8] = np.asarray(inp["ffn_post_norm"][0]).reshape(8, 128).T
    cst[:, C_GPLE:C_GPLE + 8] = np.asarray(inp["ple_norm"][0]).reshape(8, 128).T
    cst[:, C_GFFNPRE:C_GFFNPRE + 8] = np.asarray(inp["ffn_pre_norm"][0]).reshape(8, 128).T
    cw = np.asarray(inp["conv_w"][0])
    for k in range(3):
        cst[:, C_CONVW + k * 4:C_CONVW + k * 4 + 4] = cw[k].reshape(4, 128).T
    half = 16
    inv_freq = (1.0 / (np.float32(10000.0) ** (np.arange(half, dtype=np.float32) / np.float32(half)))).astype(np.float32)
    pidx = np.arange(128)
    cst[:, C_INVF] = inv_freq[pidx % 16]
    cst[:, C_PHASE] = np.where((pidx // 32) % 2 == 0, np.float32(math.pi / 2), np.float32(0.0))
    return wsrc, gsrc, cst


_NC_CACHE = {}


def kernel(**inputs):
    inp = {k: np.asarray(v) for k, v in inputs.items()}
    wsrc, gsrc, cst = host_layout(inp)
    x = inp["x"]
    p = inp["p"][0]
    pos = inp["positions"].astype(np.int32)
    if "nc" not in _NC_CACHE:
        _NC_CACHE["nc"] = build_nc()
    nc = _NC_CACHE["nc"]
    in_maps = []
    for i in range(NCORES):
        sl = slice(i * NSEQ, (i + 1) * NSEQ)
        in_maps.append({"x": np.ascontiguousarray(x[sl]), "p": np.ascontiguousarray(p[sl]),
                        "pos": np.ascontiguousarray(pos[sl]), "wsrc": wsrc, "gsrc": gsrc, "cst": cst})
    res = run_bass_kernel_spmd(nc, in_maps, core_ids=list(range(NCORES)))
    return np.concatenate([r["out"] for r in res.results], axis=0).astype(np.float32)
```
